# Optimizing a Trainium2 kernel written in Bass

```python
import math
import jax, jax.numpy as jnp
from jax import lax
import numpy as np

D_MODEL = 4096
BATCH = 4
SEQ = 4096
DEPTH = 2
DEC_BATCH = 4
DEC_SEQ = 2048
PAST_LEN = 128

C_ATT = D_MODEL // 2
C_RWKV = D_MODEL - C_ATT
ATT_DK = 64
ATT_DV = 2 * ATT_DK
N_ATT_HEADS = C_ATT // ATT_DV
RWKV_N = 64
N_RWKV_HEADS = C_RWKV // RWKV_N
R_W = max(32, int(round(1.8 * C_RWKV ** 0.5 / 32)) * 32)
R_A = max(32, int(round(1.8 * C_RWKV ** 0.5 / 32)) * 32)
R_G = max(32, int(round(0.6 * C_RWKV ** 0.8 / 32)) * 32)
RWKV_COLS = 3 * C_RWKV + R_W + R_A + R_G
IN_COLS = 3 * C_ATT + RWKV_COLS
N_BUCKETS = 32
MAX_DISTANCE = 128
QB = 128
D_FF = ((8 * D_MODEL // 3 + 255) // 256) * 256
CONV_W = 3
RMS_EPS = 1e-6
SUBLN_EPS = 1e-5
GN_EPS = 64e-5

kernel_name = 'hybrid_diffattn_rwkv7_encoder'


def rms_norm(x, g, eps=RMS_EPS):
    xf = x.astype(jnp.float32)
    y = xf * lax.rsqrt(jnp.mean(xf * xf, axis=-1, keepdims=True) + eps)
    return (y * g.astype(jnp.float32)).astype(x.dtype)


def shift_seq(z):
    zp = jnp.pad(z, ((0, 0), (1, 1), (0, 0)))
    return zp[:, :-2], zp[:, 2:]


def rel_bucket(rel):
    nb = N_BUCKETS // 2
    max_exact = nb // 2
    n = jnp.abs(rel)
    nf = jnp.maximum(n, 1).astype(jnp.float32)
    large = max_exact + (jnp.log(nf / max_exact) / math.log(MAX_DISTANCE / max_exact)
                         * (nb - max_exact)).astype(jnp.int32)
    large = jnp.minimum(large, nb - 1)
    return jnp.where(rel > 0, nb, 0) + jnp.where(n < max_exact, n, large)


def diff_attention(z_att, lam_p, subln_g, rel_bias, layer_idx):
    B, S, _ = z_att.shape
    q, k, v = jnp.split(z_att, 3, axis=-1)
    q = q.reshape(B, S, N_ATT_HEADS, 2, ATT_DK)
    k = k.reshape(B, S, N_ATT_HEADS, 2, ATT_DK)
    v = v.reshape(B, S, N_ATT_HEADS, ATT_DV)
    lam_init = 0.8 - 0.6 * math.exp(-0.3 * layer_idx)
    lp = lam_p.astype(jnp.float32)
    lam = jnp.exp(jnp.dot(lp[0], lp[1])) - jnp.exp(jnp.dot(lp[2], lp[3])) + lam_init
    nblk = S // QB
    qb = q.reshape(B, nblk, QB, N_ATT_HEADS, 2, ATT_DK).transpose(1, 0, 2, 3, 4, 5)
    k_pos = jnp.arange(S, dtype=jnp.int32)
    scale = ATT_DK ** -0.5
    table = rel_bias.astype(jnp.float32)
    g = subln_g.astype(jnp.float32)

    def block(args):
        q_blk, start = args
        q_pos = start + jnp.arange(QB, dtype=jnp.int32)
        bias = table[rel_bucket(k_pos[None, :] - q_pos[:, None])].transpose(2, 0, 1)
        s = jnp.einsum('bqhcd,bkhcd->bhcqk', q_blk, k).astype(jnp.float32) * scale + bias[None, :, None]
        p = jax.nn.softmax(s, axis=-1)
        a = p[:, :, 0] - lam * p[:, :, 1]
        o = jnp.einsum('bhqk,bkhe->bqhe', a.astype(v.dtype), v).astype(jnp.float32)
        o = o * lax.rsqrt(jnp.mean(o * o, axis=-1, keepdims=True) + SUBLN_EPS) * g
        return (o * (1.0 - lam_init)).astype(z_att.dtype)

    starts = jnp.arange(nblk, dtype=jnp.int32) * QB
    out = lax.map(block, (qb, starts))
    return out.transpose(1, 0, 2, 3, 4).reshape(B, S, C_ATT)


def rwkv7_bidirectional(z, mu, w0, w_up, a0, a_up, g_up, k_k, k_a, r_k, ln_g, ln_b):
    B, S, _ = z.shape
    H, N = N_RWKV_HEADS, RWKV_N
    f32 = jnp.float32
    prev, nxt = shift_seq(z)
    z = z + mu[0] * (prev - z) + mu[1] * (nxt - z)
    i1, i2, i3 = C_RWKV, 2 * C_RWKV, 3 * C_RWKV
    r, k, v = z[..., :i1], z[..., i1:i2], z[..., i2:i3]
    xw = z[..., i3:i3 + R_W]
    xa = z[..., i3 + R_W:i3 + R_W + R_A]
    xg = z[..., i3 + R_W + R_A:]
    w_log = -jax.nn.softplus(-(w0[:, None, None, :] + jnp.einsum('bsr,drc->dbsc', jnp.tanh(xw), w_up)).astype(f32)) - 0.5
    decay = jnp.exp(-jnp.exp(w_log))
    a = jax.nn.sigmoid((a0[:, None, None, :] + jnp.einsum('bsr,drc->dbsc', xa, a_up)).astype(f32))
    g = jnp.einsum('bsr,rc->bsc', jax.nn.sigmoid(xg), g_up).astype(f32)
    rf, kf, vf = r.astype(f32), k.astype(f32), v.astype(f32)
    kk = (kf * k_k.astype(f32)).reshape(B, S, H, N)
    kk = kk / jnp.maximum(jnp.linalg.norm(kk, axis=-1, keepdims=True), 1e-12)
    k_dir = kf[None] * (1.0 + (a - 1.0) * k_a.astype(f32))

    def both(t):
        return jnp.stack([t, jnp.flip(t, 1)])

    def flip_bwd(t):
        return jnp.stack([t[0], jnp.flip(t[1], 1)]).reshape(2, B, S, H, N)

    seqs = (both(rf.reshape(B, S, H, N)), flip_bwd(decay), flip_bwd(k_dir),
            both(vf.reshape(B, S, H, N)), both(kk), flip_bwd(a))
    seqs = tuple(jnp.moveaxis(t, 2, 0) for t in seqs)

    def step(state, inp):
        r_t, w_t, k_t, v_t, kk_t, a_t = inp
        sa = jnp.einsum('dbhvk,dbhk->dbhv', state, -kk_t)
        state = (state * w_t[..., None, :] + sa[..., :, None] * (kk_t * a_t)[..., None, :]
                 + v_t[..., :, None] * k_t[..., None, :])
        return state, jnp.einsum('dbhvk,dbhk->dbhv', state, r_t)

    s0 = jnp.zeros((2, B, H, N, N), f32)
    _, y = lax.scan(step, s0, seqs)
    y = jnp.moveaxis(y, 0, 2)
    y = y[0] + jnp.flip(y[1], 1)
    mean = jnp.mean(y, axis=-1, keepdims=True)
    var = jnp.mean((y - mean) ** 2, axis=-1, keepdims=True)
    y = ((y - mean) * lax.rsqrt(var + GN_EPS)).reshape(B, S, C_RWKV) * ln_g.astype(f32) + ln_b.astype(f32)
    bonus = jnp.sum((rf * k_dir.sum(0) * r_k.astype(f32).reshape(-1)).reshape(B, S, H, N), axis=-1, keepdims=True)
    y = y + (bonus * vf.reshape(B, S, H, N)).reshape(B, S, C_RWKV)
    return (y * g).astype(z.dtype)


def conv_glu(h, w_up, w_conv, w_down):
    u = h @ w_up
    prev, nxt = shift_seq(u)
    u = prev * w_conv[0] + u * w_conv[1] + nxt * w_conv[2]
    gate, val = u[..., :D_FF], u[..., D_FF:]
    return (jax.nn.silu(gate) * val) @ w_down


def encoder_layer(x, c, l, p):
    B, S, D = x.shape
    mod = (jax.nn.silu(c) @ p['ada_w'][l] + p['ada_b'][l]).reshape(B, 6, 1, D)
    shift1, scale1, gate1, shift2, scale2, gate2 = (mod[:, i] for i in range(6))
    h = rms_norm(x, p['norm1_g'][l]) * (1 + scale1) + shift1
    z = h @ p['w_in'][l]
    o_att = diff_attention(z[..., :3 * C_ATT], p['att_lambda'][l], p['att_subln_g'][l], p['rel_bias'], l)
    o_rwkv = rwkv7_bidirectional(z[..., 3 * C_ATT:], p['rwkv_mu'][l], p['rwkv_w0'][l], p['rwkv_w_up'][l],
                                 p['rwkv_a0'][l], p['rwkv_a_up'][l], p['rwkv_g_up'][l], p['rwkv_k_k'][l],
                                 p['rwkv_k_a'][l], p['rwkv_r_k'][l], p['rwkv_ln_g'][l], p['rwkv_ln_b'][l])
    x = x + gate1 * (jnp.concatenate([o_att, o_rwkv], axis=-1) @ p['w_out'][l])
    h = rms_norm(x, p['norm2_g'][l]) * (1 + scale2) + shift2
    x = x + gate2 * conv_glu(h, p['ffn_up'][l], p['ffn_conv'][l], p['ffn_down'][l])
    return x


def run_trunk(x, c, p):
    for l in range(DEPTH):
        x = encoder_layer(x, c, l, p)
    return rms_norm(x, p['final_g'])


def setup_inputs(seed: int = 0) -> dict:
    key = jax.random.key(seed)
    ks = list(jax.random.split(key, 40))
    f32 = jnp.float32

    def nrm(shape, scale):
        return jax.random.normal(ks.pop(), shape, f32) * scale

    def uni(shape, lo, hi):
        return jax.random.uniform(ks.pop(), shape, f32, lo, hi)

    L, D = DEPTH, D_MODEL
    conv_base = jnp.array([0.25, 1.0, 0.25], f32)[None, :, None]
    return {
        'x_prompt': nrm((BATCH, SEQ, D), 1.0),
        'x_sample': nrm((DEC_BATCH, DEC_SEQ, D), 1.0),
        'c_prompt': nrm((BATCH, D), 1.0),
        'c_sample': nrm((DEC_BATCH, D), 1.0),
        'ada_w': nrm((L, D, 6 * D), 0.5 * D ** -0.5),
        'ada_b': nrm((L, 6 * D), 0.02),
        'norm1_g': 1.0 + nrm((L, D), 0.02),
        'w_in': nrm((L, D, IN_COLS), D ** -0.5),
        'att_lambda': nrm((L, 4, ATT_DK), 0.1),
        'att_subln_g': 1.0 + nrm((L, ATT_DV), 0.02),
        'rel_bias': nrm((N_BUCKETS, N_ATT_HEADS), 0.5),
        'rwkv_mu': uni((L, 2, RWKV_COLS), 0.0, 0.5),
        'rwkv_w0': uni((L, 2, C_RWKV), -6.0, -0.5),
        'rwkv_w_up': nrm((L, 2, R_W, C_RWKV), 0.5 * R_W ** -0.5),
        'rwkv_a0': nrm((L, 2, C_RWKV), 0.5),
        'rwkv_a_up': nrm((L, 2, R_A, C_RWKV), 0.5 * R_A ** -0.5),
        'rwkv_g_up': nrm((L, R_G, C_RWKV), R_G ** -0.5),
        'rwkv_k_k': 0.85 + nrm((L, C_RWKV), 0.05),
        'rwkv_k_a': 1.0 + nrm((L, C_RWKV), 0.05),
        'rwkv_r_k': nrm((L, N_RWKV_HEADS, RWKV_N), 0.1),
        'rwkv_ln_g': 1.0 + nrm((L, C_RWKV), 0.02),
        'rwkv_ln_b': nrm((L, C_RWKV), 0.02),
        'w_out': nrm((L, D, D), D ** -0.5),
        'norm2_g': 1.0 + nrm((L, D), 0.02),
        'ffn_up': nrm((L, D, 2 * D_FF), D ** -0.5),
        'ffn_conv': conv_base + nrm((L, CONV_W, 2 * D_FF), 0.1),
        'ffn_down': nrm((L, D_FF, D), D_FF ** -0.5),
        'final_g': 1.0 + nrm((D,), 0.02),
    }


def reference(x_prompt, x_sample, c_prompt, c_sample, ada_w, ada_b, norm1_g, w_in, att_lambda,
              att_subln_g, rel_bias, rwkv_mu, rwkv_w0, rwkv_w_up, rwkv_a0, rwkv_a_up, rwkv_g_up,
              rwkv_k_k, rwkv_k_a, rwkv_r_k, rwkv_ln_g, rwkv_ln_b, w_out, norm2_g, ffn_up, ffn_conv,
              ffn_down, final_g):
    p = dict(ada_w=ada_w, ada_b=ada_b, norm1_g=norm1_g, w_in=w_in, att_lambda=att_lambda,
             att_subln_g=att_subln_g, rel_bias=rel_bias, rwkv_mu=rwkv_mu, rwkv_w0=rwkv_w0,
             rwkv_w_up=rwkv_w_up, rwkv_a0=rwkv_a0, rwkv_a_up=rwkv_a_up, rwkv_g_up=rwkv_g_up,
             rwkv_k_k=rwkv_k_k, rwkv_k_a=rwkv_k_a, rwkv_r_k=rwkv_r_k, rwkv_ln_g=rwkv_ln_g,
             rwkv_ln_b=rwkv_ln_b, w_out=w_out, norm2_g=norm2_g, ffn_up=ffn_up, ffn_conv=ffn_conv,
             ffn_down=ffn_down, final_g=final_g)
    y_prompt = run_trunk(x_prompt, c_prompt, p)
    y_sample = run_trunk(x_sample, c_sample, p)
    return (y_prompt, y_sample)
```

```python
import math
from contextlib import ExitStack
import numpy as np
import concourse.bass as bass
import concourse.mybir as mybir
from concourse.bass_utils import run_bass_kernel_spmd

F32 = mybir.dt.float32
BF16 = mybir.dt.bfloat16
ALU = mybir.AluOpType
AF = mybir.ActivationFunctionType
AX = mybir.AxisListType

D = 4096
T = 4096
NL = 2
KC = D // 128
C_ATT = 2048
C_RWKV = 2048
NH_ATT = 16
NH_RWKV = 32
R_W = 96
R_A = 96
R_G = 256
RWKV_COLS = 3 * C_RWKV + R_W + R_A + R_G
IN_COLS = 3 * C_ATT + RWKV_COLS
D_FF = 11008
NFF = D_FF // 128
RMS_EPS = 1e-6
SUBLN_EPS = 1e-5
GN_EPS = 64e-5
NT = T // 128


class Src:
    def __init__(self, name, sem):
        self.name = name
        self.sem = sem
        self.count = 0


class Eng(Src):
    def __init__(self, name, sem, eng):
        super().__init__(name, sem)
        self.eng = eng
        self.waited = {}

    def wait_for(self, src, val, raw=False):
        if val <= 0:
            return
        if src is self:
            if not raw or self.name == "pe" or val > self.count:
                return
        if self.waited.get(src, 0) >= val:
            return
        self.eng.wait_ge(src.sem, val)
        self.waited[src] = val


class Buf:
    __slots__ = ("w", "r", "name", "shared", "ws")

    def __init__(self, name="", shared=False):
        self.w = None
        self.r = {}
        self.ws = {}
        self.name = name
        self.shared = shared


class K:
    def __init__(self, nc, es):
        self.nc = nc
        self.es = es

        def sem(n):
            return es.enter_context(nc.semaphore(n))

        self.PE = Eng("pe", sem("s_pe"), nc.tensor)
        self.ACT = Eng("act", sem("s_act"), nc.scalar)
        self.DVE = Eng("dve", sem("s_dve"), nc.vector)
        self.POOL = Eng("pool", sem("s_pool"), nc.gpsimd)
        self.SP = Eng("sp", sem("s_sp"), nc.sync)
        self.lanes = {
            self.SP: [Src(f"lsp{i}", sem(f"s_lsp{i}")) for i in range(20)],
            self.POOL: [Src(f"lpl{i}", sem(f"s_lpl{i}")) for i in range(8)],
            self.ACT: [Src(f"lac{i}", sem(f"s_lac{i}")) for i in range(8)],
        }
        self.lane_rr = {self.SP: 0, self.POOL: 0, self.ACT: 0}
        self.n_ins = 0
        self.uid = 0

    def _deps(self, E, r, w):
        for b in r:
            if b.shared:
                for s_, v in b.ws.items():
                    E.wait_for(s_, v)
            elif b.w is not None:
                E.wait_for(*b.w, raw=True)
        for b in w:
            if not b.shared and b.w is not None:
                E.wait_for(*b.w)
            for s_, v in b.r.items():
                E.wait_for(s_, v)

    def _record(self, src, val, r, w):
        for b in r:
            if b.r.get(src, 0) < val:
                b.r[src] = val
        for b in w:
            if b.shared:
                if b.ws.get(src, 0) < val:
                    b.ws[src] = val
            else:
                b.w = (src, val)
                b.r = {}

    def op(self, E, fn, r=(), w=(), inc=True):
        self._deps(E, r, w)
        ins = fn(E.eng)
        self.n_ins += 1
        if inc:
            ins.then_inc(E.sem, 1)
            E.count += 1
            val = E.count
        else:
            val = E.count + 1
        self._record(E, val, r, w)
        return ins

    def dma(self, out, in_, r=(), w=(), q=None, **kw):
        Q = q or self.SP
        self._deps(Q, r, w)
        lanes = self.lanes[Q]
        lane = lanes[self.lane_rr[Q] % len(lanes)]
        self.lane_rr[Q] += 1
        Q.wait_for(lane, lane.count)
        Q.eng.dma_start(out=out, in_=in_, **kw).then_inc(lane.sem, 16)
        self.n_ins += 1
        lane.count += 16
        self._record(lane, lane.count, r, w)

    def barrier(self):
        srcs = [self.PE, self.ACT, self.DVE, self.POOL]
        for lanes in self.lanes.values():
            srcs.extend(lanes)
        for E in (self.PE, self.ACT, self.DVE, self.POOL, self.SP):
            for s_ in srcs:
                E.wait_for(s_, s_.count)

    def finish(self):
        for Q, lanes in self.lanes.items():
            for lane in lanes:
                self.SP.wait_for(lane, lane.count)
        for E in (self.PE, self.ACT, self.DVE, self.POOL):
            self.SP.wait_for(E, E.count)

    def sb(self, es, name, shape, dt):
        self.uid += 1
        return es.enter_context(self.nc.sbuf_tensor(f"{name}_u{self.uid}", list(shape), dt))

    def ps(self, es, name, shape, dt=F32):
        self.uid += 1
        return es.enter_context(self.nc.psum_tensor(f"{name}_u{self.uid}", list(shape), dt))


class Ctx:
    pass


def build_program(dbg=None, nlayers=NL, phases=None, only_inputs=None, scan_steps=None, only_scratch=None, scan_cut=99):
    dbg = dbg or set()
    nc = bass.Bass("TRN2", target_bir_lowering=False)
    g = Ctx()
    g.nc = nc
    g.bufs = {}
    g.scan_steps = scan_steps
    g.scan_cut = scan_cut

    def din(name, shape, dt=F32):
        if only_inputs is not None and name not in only_inputs:
            return None
        return nc.dram_tensor(name, list(shape), dt, kind="ExternalInput").ap()

    def dscr(name, shape, dt=F32):
        if only_scratch is not None and name not in only_scratch:
            return None
        kind = "ExternalOutput" if name in dbg else "Internal"
        return nc.dram_tensor(name, list(shape), dt, kind=kind).ap()

    g.x = din("x", [T, D])
    g.c = din("c", [D])
    g.tmask = din("tmask", [T])
    g.ident = din("ident", [128, 128])
    g.ada_w = din("ada_w", [NL, D, 6 * D])
    g.ada_b = din("ada_b", [NL, 6 * D])
    g.norm1_g = din("norm1_g", [NL, D])
    g.w_in = din("w_in", [NL, D, IN_COLS])
    g.bkconst = din("bkconst", [128, 1152])
    g.att_lambda = din("att_lambda", [NL, 4, 64])
    g.att_subln_g = din("att_subln_g", [NL, 128])
    g.rel_bias = din("rel_bias", [32, 16])
    g.w_out = din("w_out", [NL, D, D])
    g.norm2_g = din("norm2_g", [NL, D])
    g.ffn_up = din("ffn_up", [NL, D, 2 * D_FF])
    g.ffn_conv = din("ffn_conv", [NL, 3, 2 * D_FF])
    g.ffn_down = din("ffn_down", [NL, D_FF, D])
    g.final_g = din("final_g", [D])
    g.rwkv_mu = din("rwkv_mu", [NL, 2, RWKV_COLS])
    g.rwkv_w0 = din("rwkv_w0", [NL, 2, C_RWKV])
    g.rwkv_w_up = din("rwkv_w_up", [NL, 2, R_W, C_RWKV])
    g.rwkv_a0 = din("rwkv_a0", [NL, 2, C_RWKV])
    g.rwkv_a_up = din("rwkv_a_up", [NL, 2, R_A, C_RWKV])
    g.rwkv_g_up = din("rwkv_g_up", [NL, R_G, C_RWKV])
    g.rwkv_k_k = din("rwkv_k_k", [NL, C_RWKV])
    g.rwkv_k_a = din("rwkv_k_a", [NL, C_RWKV])
    g.rwkv_r_k = din("rwkv_r_k", [NL, NH_RWKV, 64])
    g.rwkv_ln_g = din("rwkv_ln_g", [NL, C_RWKV])
    g.rwkv_ln_b = din("rwkv_ln_b", [NL, C_RWKV])
    g.onesbd = din("onesbd", [128, 128])
    g.trimask = din("trimask", [128, 4, 128])
    g.y = nc.dram_tensor("y", [T, D], F32, kind="ExternalOutput").ap()

    g.modb = dscr("modb", [6, 128, D])
    g.hT = dscr("hT", [KC, 128, T + 2], BF16)
    g.qkT = dscr("qkT", [32, 128, T], BF16)
    g.vtok = dscr("vtok", [T, C_ATT], BF16)
    g.zrT = dscr("zrT", [6656, T + 2], F32)
    g.Gd = dscr("Gd", [NH_ATT, 128, 1152])
    g.mixT = dscr("mixT", [KC, 128, T], BF16)
    g.aT = dscr("aT", [NFF, 128, T], BF16)
    g.rk_ops = dscr("rk_ops", [2, 6, 16, 128, T], BF16)
    g.rk_gc = dscr("rk_gc", [2, 16, 128, NT])
    g.rk_vT = dscr("rk_vT", [16, 128, T], BF16)
    g.rk_g = dscr("rk_g", [16, 128, T])
    g.rk_bv = dscr("rk_bv", [16, 128, T])
    g.rk_y = dscr("rk_y", [2, T, C_RWKV])
    g.xa = dscr("xa", [T, D])
    g.xb = dscr("xb", [T, D])

    with ExitStack() as es:
        k = K(nc, es)
        g.k = k
        ph = phases or {"ada", "norm1", "win", "attn", "rwkv", "wout", "norm2", "ffnup", "ffndown", "final"}
        bx = g.bufs.setdefault('x', Buf('x', True))
        bxa = g.bufs.setdefault('xa', Buf('xa', True))
        bxb = g.bufs.setdefault('xb', Buf('xb', True))
        if "attn" in ph:
            phase_attn_bias(g)
        xcur, bcur = g.x, bx
        for l in range(nlayers):
            if "ada" in ph:
                phase_ada(g, l)
            if "norm1" in ph:
                phase_norm(g, l, xcur, bcur, g.norm1_g, 0, zero_halo=(l == 0))
            if "win" in ph:
                phase_win(g, l)
            if "attn" in ph:
                phase_attn(g, l)
            if "rwkv" in ph or "rwkv_prep" in ph:
                phase_rwkv_prep(g, l)
            if "rwkv" in ph or "rwkv_scan" in ph:
                phase_rwkv_scan(g, l)
            if "rwkv" in ph or "rwkv_post" in ph:
                phase_rwkv_post(g, l)
            if "wout" in ph:
                phase_reslinear(g, l, g.mixT, g.bufs.setdefault("mixT", Buf("mixT", True)), KC, 0, g.w_out[l], 2,
                                xcur, bcur, g.xa, bxa)
            if "norm2" in ph:
                phase_norm(g, l, g.xa, bxa, g.norm2_g, 1)
            if "ffnup" in ph:
                phase_ffnup(g, l)
            if "ffndown" in ph:
                phase_reslinear(g, l, g.aT, g.bufs.setdefault("aT", Buf("aT", True)), NFF, 0, g.ffn_down[l], 5,
                                g.xa, bxa, g.xb, bxb)
            xcur, bcur = g.xb, bxb
        if "final" in ph:
            phase_final(g, xcur, bcur)
        k.finish()
        g.n_ins = k.n_ins
    nc._n_ins = k.n_ins
    return nc


def phase_ada(g, l):
    nc, k = g.nc, g.k
    k.barrier()
    with ExitStack() as es:
        craw = k.sb(es, "ada_c", [128, KC], F32)
        sc = k.sb(es, "ada_sc", [128, KC], F32)
        scb = k.sb(es, "ada_scb", [128, KC, 128], F32)
        wt = [k.sb(es, f"ada_w{i}", [128, 8, 512], F32) for i in range(2)]
        bb = [k.sb(es, f"ada_b{i}", [128, 512], F32) for i in range(2)]
        ot = [k.sb(es, f"ada_o{i}", [128, 512], F32) for i in range(2)]
        pst = [k.ps(es, f"ada_ps{i}", [128, 512]) for i in range(2)]
        b_c, b_sc, b_scb = Buf(), Buf(), Buf()
        b_wt = [Buf(), Buf()]
        b_bb = [Buf(), Buf()]
        b_ot = [Buf(), Buf()]
        b_ps = [Buf(), Buf()]
        b_modb = g.bufs.setdefault("modb", Buf("modb", True))
        k.dma(craw[:], g.c.rearrange("(kc p) -> p kc", p=128), w=[b_c], allow_slow_non_contiguous=True)
        k.op(k.ACT, lambda e: e.activation(sc[:], craw[:], AF.Silu), r=[b_c], w=[b_sc])
        k.op(k.DVE, lambda e: e.tensor_copy(scb[:], sc[:].unsqueeze(2).to_broadcast([128, KC, 128])),
             r=[b_sc], w=[b_scb])
        wv = g.ada_w[l].rearrange("(kc p) n -> p kc n", p=128)
        NB = 6 * D // 512
        for nb in range(NB):
            i = nb % 2
            cs = slice(nb * 512, (nb + 1) * 512)
            k.dma(bb[i][:], g.ada_b[l:l + 1, cs].partition_broadcast(128), w=[b_bb[i]])
            for q4 in range(4):
                wi = (nb * 4 + q4) % 2
                k.dma(wt[wi][:], wv[:, q4 * 8:(q4 + 1) * 8, cs], w=[b_wt[wi]])
                for kk in range(8):
                    kc = q4 * 8 + kk
                    last = (kc == KC - 1)
                    k.op(k.PE, lambda e: e.matmul(pst[i][:], scb[:, kc, :], wt[wi][:, kk, :],
                                                  start=(kc == 0), stop=last),
                         r=[b_scb, b_wt[wi]], w=[b_ps[i]], inc=(last or kk == 7))
            k.op(k.DVE, lambda e: e.tensor_tensor(ot[i][:], pst[i][:], bb[i][:], ALU.add),
                 r=[b_ps[i], b_bb[i]], w=[b_ot[i]])
            j, off = divmod(nb * 512, D)
            k.dma(g.modb[j, :, off:off + 512], ot[i][:], r=[b_ot[i]], w=[b_modb])


def phase_norm(g, l, xsrc, b_x, gvec, which, zero_halo=False):
    nc, k = g.nc, g.k
    b_modb = g.bufs.setdefault("modb", Buf("modb", True))
    b_hT = g.bufs.setdefault("hT", Buf("hT", True))
    k.barrier()
    with ExitStack() as es:
        G = k.sb(es, "nm_G", [128, D], F32)
        SH = k.sb(es, "nm_SH", [128, D], F32)
        gb = k.sb(es, "nm_gb", [128, D], F32)
        idf = k.sb(es, "nm_idf", [128, 128], F32)
        idb = k.sb(es, "nm_idb", [128, 128], BF16)
        xt = [k.sb(es, f"nm_x{i}", [128, D], F32) for i in range(2)]
        junk = k.sb(es, "nm_junk", [128, D], BF16)
        tmp = k.sb(es, "nm_tmp", [128, D], F32)
        hb = [k.sb(es, f"nm_hb{i}", [128, D], BF16) for i in range(2)]
        hTt = [k.sb(es, f"nm_hT{i}", [128, KC, 128], BF16) for i in range(2)]
        ss = [k.sb(es, f"nm_ss{i}", [128, 1], F32) for i in range(2)]
        rs = [k.sb(es, f"nm_rs{i}", [128, 1], F32) for i in range(2)]
        pst = [k.ps(es, f"nm_ps{i}", [128, 1024], BF16) for i in range(4)]
        b_G, b_SH, b_gb, b_idf, b_idb, b_junk, b_tmp = (Buf() for _ in range(7))
        b_xt = [Buf(), Buf()]
        b_hb = [Buf(), Buf()]
        b_hTt = [Buf(), Buf()]
        b_ss = [Buf(), Buf()]
        b_rs = [Buf(), Buf()]
        b_ps = [Buf() for _ in range(4)]
        k.dma(gb[:], gvec[l:l + 1, :].partition_broadcast(128), w=[b_gb])
        k.dma(G[:], g.modb[3 * which + 1], r=[b_modb], w=[b_G])
        k.dma(SH[:], g.modb[3 * which], r=[b_modb], w=[b_SH])
        k.dma(idf[:], g.ident, w=[b_idf])
        tmk = k.sb(es, "nm_tmk", [128, NT], F32)
        b_tmk = Buf()
        k.dma(tmk[:], g.tmask.rearrange("(i p) -> p i", p=128), w=[b_tmk], allow_slow_non_contiguous=True)
        k.op(k.DVE, lambda e: e.tensor_copy(idb[:], idf[:]), r=[b_idf], w=[b_idb])
        k.op(k.DVE, lambda e: e.scalar_tensor_tensor(G[:], G[:], 1.0, gb[:], ALU.add, ALU.mult),
             r=[b_G, b_gb], w=[b_G])
        if zero_halo:
            k.op(k.DVE, lambda e: e.memset(junk[:, 0:KC], 0.0), w=[b_junk])
            for col in (0, T + 1):
                k.dma(g.hT[:, :, col:col + 1].rearrange("kc p t -> p kc t"), junk[:, 0:KC].unsqueeze(2),
                      r=[b_junk], w=[b_hT], allow_slow_non_contiguous=True)
        for i in range(NT):
            j = i % 2
            k.dma(xt[j][:], xsrc[i * 128:(i + 1) * 128, :], r=[b_x], w=[b_xt[j]])
            k.op(k.ACT, lambda e: e.activation(junk[:], xt[j][:], AF.Square, accum_out=ss[j][:]),
                 r=[b_xt[j]], w=[b_junk, b_ss[j]])
            k.op(k.DVE, lambda e: e.tensor_scalar(rs[j][:], ss[j][:], 1.0 / D, RMS_EPS, ALU.mult, ALU.add),
                 r=[b_ss[j]], w=[b_rs[j]])
            k.op(k.ACT, lambda e: e.sqrt(rs[j][:], rs[j][:]), r=[b_rs[j]], w=[b_rs[j]])
            k.op(k.DVE, lambda e: e.reciprocal(rs[j][:], rs[j][:]), r=[b_rs[j]], w=[b_rs[j]])
            k.op(k.DVE, lambda e: e.tensor_tensor(rs[j][:], rs[j][:], tmk[:, i:i + 1], ALU.mult),
                 r=[b_rs[j], b_tmk], w=[b_rs[j]])
            k.op(k.DVE, lambda e: e.scalar_tensor_tensor(tmp[:], xt[j][:], rs[j][:], G[:], ALU.mult, ALU.mult),
                 r=[b_xt[j], b_rs[j], b_G], w=[b_tmp])
            k.op(k.DVE, lambda e: e.scalar_tensor_tensor(hb[j][:], SH[:], tmk[:, i:i + 1], tmp[:], ALU.mult, ALU.add),
                 r=[b_tmp, b_SH, b_tmk], w=[b_hb[j]])
            for q in range(4):
                pi = (i * 4 + q) % 4
                for kk in range(8):
                    kc = q * 8 + kk
                    k.op(k.PE, lambda e: e.transpose(pst[pi][:, kk * 128:(kk + 1) * 128],
                                                     hb[j][:, kc * 128:(kc + 1) * 128], idb[:]),
                         r=[b_hb[j], b_idb], w=[b_ps[pi]], inc=(kk == 7))
                E = k.ACT if q % 2 == 0 else k.DVE
                if E is k.ACT:
                    k.op(E, lambda e: e.copy(hTt[j][:, q * 8:(q + 1) * 8, :], pst[pi][:].rearrange("p (a b) -> p a b", a=8)),
                         r=[b_ps[pi]], w=[b_hTt[j]])
                else:
                    k.op(E, lambda e: e.tensor_copy(hTt[j][:, q * 8:(q + 1) * 8, :], pst[pi][:].rearrange("p (a b) -> p a b", a=8)),
                         r=[b_ps[pi]], w=[b_hTt[j]])
            k.dma(g.hT[:, :, 1 + i * 128:1 + (i + 1) * 128].rearrange("kc p t -> p kc t"), hTt[j][:],
                  r=[b_hTt[j]], w=[b_hT])


def phase_win(g, l):
    nc, k = g.nc, g.k
    b_hT = g.bufs.setdefault("hT", Buf("hT", True))
    b_qkT = g.bufs.setdefault("qkT", Buf("qkT", True))
    b_vtok = g.bufs.setdefault("vtok", Buf("vtok", True))
    b_zrT = g.bufs.setdefault("zrT", Buf("zrT", True))
    wv = g.w_in[l].rearrange("(kc p) n -> p kc n", p=128)
    segs = []
    for i in range(8):
        segs.append(("qk", i * 512, [128] * 4))
    for i in range(4):
        segs.append(("v", 4096 + i * 512, [512]))
    for i in range(12):
        segs.append(("r", 6144 + i * 512, [128] * 4))
    segs.append(("r", 12288, [96, 96, 128, 128]))
    TB = 1024
    k.barrier()
    with ExitStack() as es:
        hTs = k.sb(es, "wi_hT", [128, KC, TB], BF16)
        wb = [k.sb(es, f"wi_w{i}", [128, KC, 512], BF16) for i in range(2)]
        of = [k.sb(es, f"wi_of{i}", [128, 512], F32) for i in range(3)]
        ob = [k.sb(es, f"wi_ob{i}", [128, 512], BF16) for i in range(3)]
        pst = [k.ps(es, f"wi_ps{i}", [128, 512]) for i in range(6)]
        b_hTs = Buf()
        b_wb = [Buf(), Buf()]
        b_of = [Buf() for _ in range(3)]
        b_ob = [Buf() for _ in range(3)]
        b_ps = [Buf() for _ in range(6)]
        zt = k.sb(es, "wi_zero", [128, 52], F32)
        b_zt = Buf()
        if l == 0:
            k.op(k.DVE, lambda e: e.memset(zt[:], 0.0), w=[b_zt])
            for col in (0, T + 1):
                k.dma(g.zrT[0:6656, col:col + 1].rearrange("(a p) t -> p a t", p=128), zt[:].unsqueeze(2),
                      r=[b_zt], w=[b_zrT], allow_slow_non_contiguous=True)
        wcount = 0
        pcount = 0
        ocount = 0
        for tb in range(T // TB):
            k.dma(hTs[:], g.hT[:, :, 1 + tb * TB:1 + (tb + 1) * TB].rearrange("kc p t -> p kc t"),
                  r=[b_hT], w=[b_hTs])
            for (kind, c0, subs) in segs:
                wi = wcount % 2
                wcount += 1
                wd = sum(subs)
                k.dma(wb[wi][:, :, 0:wd], wv[:, :, c0:c0 + wd], w=[b_wb[wi]], q=k.POOL)
                if kind == "v":
                    for tt in range(TB // 128):
                        pi = pcount % 6
                        pcount += 1
                        for kc in range(KC):
                            k.op(k.PE, lambda e: e.matmul(pst[pi][:], hTs[:, kc, tt * 128:(tt + 1) * 128],
                                                          wb[wi][:, kc, :], start=(kc == 0), stop=(kc == KC - 1)),
                                 r=[b_hTs, b_wb[wi]], w=[b_ps[pi]], inc=(kc == KC - 1))
                        oi = ocount % 3
                        ocount += 1
                        if oi % 2 == 0:
                            k.op(k.ACT, lambda e: e.copy(ob[oi][:], pst[pi][:]), r=[b_ps[pi]], w=[b_ob[oi]])
                        else:
                            k.op(k.DVE, lambda e: e.tensor_copy(ob[oi][:], pst[pi][:]), r=[b_ps[pi]], w=[b_ob[oi]])
                        t0 = tb * TB + tt * 128
                        k.dma(g.vtok[t0:t0 + 128, c0 - 4096:c0 - 4096 + 512], ob[oi][:], r=[b_ob[oi]], w=[b_vtok])
                    continue
                off = 0
                for m in subs:
                    p0 = pcount % 6
                    p1 = (pcount + 1) % 6
                    pcount += 2
                    pp = (p0, p1)
                    for kc in range(KC):
                        for th in range(2):
                            k.op(k.PE, lambda e: e.matmul(pst[pp[th]][0:m, :], wb[wi][:, kc, off:off + m],
                                                          hTs[:, kc, th * 512:(th + 1) * 512],
                                                          start=(kc == 0), stop=(kc == KC - 1)),
                                 r=[b_hTs, b_wb[wi]], w=[b_ps[pp[th]]], inc=(kc == KC - 1))
                    for th in range(2):
                        oi = ocount % 3
                        ocount += 1
                        t0 = tb * TB + th * 512
                        if kind == "qk":
                            dst, bdst = ob[oi], b_ob[oi]
                            dram = g.qkT[(c0 + off) // 128, :, t0:t0 + 512]
                            bd = b_qkT
                        else:
                            dst, bdst = of[oi], b_of[oi]
                            r0 = c0 + off - 6144
                            dram = g.zrT[r0:r0 + m, 1 + t0:1 + t0 + 512]
                            bd = b_zrT
                        if oi % 2 == 0:
                            k.op(k.ACT, lambda e: e.copy(dst[0:m, :], pst[pp[th]][0:m, :]), r=[b_ps[pp[th]]], w=[bdst])
                        else:
                            k.op(k.DVE, lambda e: e.tensor_copy(dst[0:m, :], pst[pp[th]][0:m, :]), r=[b_ps[pp[th]]], w=[bdst])
                        k.dma(dram, dst[0:m, :], r=[bdst], w=[bd])
                    off += m


def phase_reslinear(g, l, srcT, b_src, nk, src_off, W, gate_j, xin, b_xin, xout, b_xout):
    nc, k = g.nc, g.k
    b_modb = g.bufs.setdefault("modb", Buf("modb", True))
    wv = W.rearrange("(kc p) n -> p kc n", p=128)
    k.barrier()
    with ExitStack() as es:
        GATE = k.sb(es, "rl_gate", [128, D], F32)
        nwb = 2 if nk <= 32 else 1
        wb = [k.sb(es, f"rl_w{i}", [128, nk, 512], BF16) for i in range(nwb)]
        st = [k.sb(es, f"rl_s{i}", [128, nk, 128], BF16) for i in range(2)]
        xi = [k.sb(es, f"rl_xi{i}", [128, 512], F32) for i in range(2)]
        xo = [k.sb(es, f"rl_xo{i}", [128, 512], F32) for i in range(2)]
        pst = [k.ps(es, f"rl_ps{i}", [128, 512]) for i in range(2)]
        b_gate = Buf()
        b_wb = [Buf() for _ in range(nwb)]
        b_st = [Buf(), Buf()]
        b_xi = [Buf(), Buf()]
        b_xo = [Buf(), Buf()]
        b_ps = [Buf(), Buf()]
        k.dma(GATE[:], g.modb[gate_j], r=[b_modb], w=[b_gate])
        cnt = 0
        for cb in range(D // 512):
            wi = cb % nwb
            k.dma(wb[wi][:], wv[:, :, cb * 512:(cb + 1) * 512], w=[b_wb[wi]], q=k.POOL)
            for tt in range(NT):
                j = cnt % 2
                cnt += 1
                t0 = tt * 128
                k.dma(st[j][:], srcT[:, :, src_off + t0:src_off + t0 + 128].rearrange("kc p t -> p kc t"),
                      r=[b_src], w=[b_st[j]])
                k.dma(xi[j][:], xin[t0:t0 + 128, cb * 512:(cb + 1) * 512], r=[b_xin], w=[b_xi[j]])
                for kc in range(nk):
                    k.op(k.PE, lambda e: e.matmul(pst[j][:], st[j][:, kc, :], wb[wi][:, kc, :],
                                                  start=(kc == 0), stop=(kc == nk - 1)),
                         r=[b_st[j], b_wb[wi]], w=[b_ps[j]], inc=(kc == nk - 1))
                k.op(k.DVE, lambda e: e.tensor_tensor(xo[j][:], pst[j][:], GATE[:, cb * 512:(cb + 1) * 512], ALU.mult),
                     r=[b_ps[j], b_gate], w=[b_xo[j]])
                k.op(k.DVE, lambda e: e.tensor_tensor(xo[j][:], xo[j][:], xi[j][:], ALU.add),
                     r=[b_xo[j], b_xi[j]], w=[b_xo[j]])
                k.dma(xout[t0:t0 + 128, cb * 512:(cb + 1) * 512], xo[j][:], r=[b_xo[j]], w=[b_xout])


def phase_ffnup(g, l):
    nc, k = g.nc, g.k
    b_hT = g.bufs.setdefault("hT", Buf("hT", True))
    b_aT = g.bufs.setdefault("aT", Buf("aT", True))
    wv = g.ffn_up[l].rearrange("(kc p) n -> p kc n", p=128)
    k.barrier()
    TB = 1024
    NS = TB // 512
    FG = 2
    with ExitStack() as es:
        hTs = k.sb(es, "fu_hT", [128, KC, NS, 514], BF16)
        wg = [k.sb(es, f"fu_wg{i}", [128, KC, FG * 128], BF16) for i in range(2)]
        wvv = [k.sb(es, f"fu_wv{i}", [128, KC, FG * 128], BF16) for i in range(2)]
        cw = k.sb(es, "fu_cw", [128, 3, 2 * NFF], F32)
        accg = [k.sb(es, f"fu_ag{i}", [128, 512], F32) for i in range(2)]
        accv = [k.sb(es, f"fu_av{i}", [128, 512], F32) for i in range(2)]
        sg = [k.sb(es, f"fu_sg{i}", [128, 512], F32) for i in range(2)]
        ao = [k.sb(es, f"fu_ao{i}", [128, 512], BF16) for i in range(2)]
        psg = [k.ps(es, f"fu_pg{i}", [128, 512]) for i in range(2)]
        psv = [k.ps(es, f"fu_pv{i}", [128, 512]) for i in range(2)]
        psh = [k.ps(es, f"fu_ph{i}", [128, 4]) for i in range(2)]
        b_hTs, b_cw = Buf(), Buf()
        b_wg = [Buf(), Buf()]
        b_wv = [Buf(), Buf()]
        b_ag = [Buf(), Buf()]
        b_av = [Buf(), Buf()]
        b_sg = [Buf(), Buf()]
        b_ao = [Buf(), Buf()]
        b_pg = [Buf(), Buf()]
        b_pv = [Buf(), Buf()]
        b_ph = [Buf(), Buf()]
        for tap in range(3):
            k.dma(cw[:, tap, :], g.ffn_conv[l, tap, :].rearrange("(j p) -> p j", p=128), w=[b_cw],
                  allow_slow_non_contiguous=True)
        wcnt = 0
        cnt = 0
        for tb in range(T // TB):
            for sidx in range(NS):
                t0 = tb * TB + sidx * 512
                k.dma(hTs[:, :, sidx, :], g.hT[:, :, t0:t0 + 514].rearrange("kc p t -> p kc t"),
                      r=[b_hT], w=[b_hTs])
            for fs in range(NFF // FG):
                wi = wcnt % 2
                wcnt += 1
                f0 = fs * FG * 128
                k.dma(wg[wi][:], wv[:, :, f0:f0 + FG * 128], w=[b_wg[wi]], q=k.POOL)
                k.dma(wvv[wi][:], wv[:, :, D_FF + f0:D_FF + f0 + FG * 128], w=[b_wv[wi]], q=k.POOL)
                for fi in range(FG):
                    f = fs * FG + fi
                    for sidx in range(NS):
                        j = cnt % 2
                        cnt += 1
                        t0 = tb * TB + sidx * 512
                        for kc in range(KC):
                            last = (kc == KC - 1)
                            lg = wg[wi][:, kc, fi * 128:(fi + 1) * 128]
                            lv = wvv[wi][:, kc, fi * 128:(fi + 1) * 128]
                            k.op(k.PE, lambda e: e.matmul(psg[j][:], lg, hTs[:, kc, sidx, 1:513], start=(kc == 0), stop=last),
                                 r=[b_hTs, b_wg[wi]], w=[b_pg[j]], inc=last)
                            k.op(k.PE, lambda e: e.matmul(psh[j][:, 0:2], lg, hTs[:, kc, sidx, 0:514:513], start=(kc == 0), stop=last),
                                 r=[b_hTs, b_wg[wi]], w=[b_ph[j]], inc=False)
                            k.op(k.PE, lambda e: e.matmul(psv[j][:], lv, hTs[:, kc, sidx, 1:513], start=(kc == 0), stop=last),
                                 r=[b_hTs, b_wv[wi]], w=[b_pv[j]], inc=last)
                            k.op(k.PE, lambda e: e.matmul(psh[j][:, 2:4], lv, hTs[:, kc, sidx, 0:514:513], start=(kc == 0), stop=last),
                                 r=[b_hTs, b_wv[wi]], w=[b_ph[j]], inc=last)
                        for (ps_, acc, b_p, b_a, col, hoff) in ((psg[j], accg[j], b_pg[j], b_ag[j], f, 0),
                                                                (psv[j], accv[j], b_pv[j], b_av[j], NFF + f, 2)):
                            w0 = cw[:, 0, col:col + 1]
                            w1 = cw[:, 1, col:col + 1]
                            w2 = cw[:, 2, col:col + 1]
                            k.op(k.ACT, lambda e: e.activation(acc[:], ps_[:], AF.Copy, scale=w1),
                                 r=[b_p, b_cw], w=[b_a])
                            k.op(k.DVE, lambda e: e.scalar_tensor_tensor(acc[:, 1:512], ps_[:, 0:511], w0, acc[:, 1:512], ALU.mult, ALU.add),
                                 r=[b_p, b_cw, b_a], w=[b_a])
                            k.op(k.DVE, lambda e: e.scalar_tensor_tensor(acc[:, 0:511], ps_[:, 1:512], w2, acc[:, 0:511], ALU.mult, ALU.add),
                                 r=[b_p, b_cw, b_a], w=[b_a])
                            k.op(k.DVE, lambda e: e.scalar_tensor_tensor(acc[:, 0:1], psh[j][:, hoff:hoff + 1], w0, acc[:, 0:1], ALU.mult, ALU.add),
                                 r=[b_ph[j], b_cw, b_a], w=[b_a])
                            k.op(k.DVE, lambda e: e.scalar_tensor_tensor(acc[:, 511:512], psh[j][:, hoff + 1:hoff + 2], w2, acc[:, 511:512], ALU.mult, ALU.add),
                                 r=[b_ph[j], b_cw, b_a], w=[b_a])
                        k.op(k.ACT, lambda e: e.activation(sg[j][:], accg[j][:], AF.Silu), r=[b_ag[j]], w=[b_sg[j]])
                        k.op(k.DVE, lambda e: e.tensor_tensor(ao[j][:], sg[j][:], accv[j][:], ALU.mult),
                             r=[b_sg[j], b_av[j]], w=[b_ao[j]])
                        k.dma(g.aT[f, :, t0:t0 + 512], ao[j][:], r=[b_ao[j]], w=[b_aT])


def phase_final(g, xin, b_xin):
    nc, k = g.nc, g.k
    b_y = g.bufs.setdefault("y", Buf("y", True))
    k.barrier()
    with ExitStack() as es:
        gb = k.sb(es, "fn_gb", [128, D], F32)
        xt = [k.sb(es, f"fn_x{i}", [128, D], F32) for i in range(2)]
        yt = [k.sb(es, f"fn_y{i}", [128, D], F32) for i in range(2)]
        junk = k.sb(es, "fn_junk", [128, D], BF16)
        ss = [k.sb(es, f"fn_ss{i}", [128, 1], F32) for i in range(2)]
        rs = [k.sb(es, f"fn_rs{i}", [128, 1], F32) for i in range(2)]
        b_gb, b_junk = Buf(), Buf()
        b_xt = [Buf(), Buf()]
        b_yt = [Buf(), Buf()]
        b_ss = [Buf(), Buf()]
        b_rs = [Buf(), Buf()]
        k.dma(gb[:], g.final_g.rearrange("(o n) -> o n", o=1).partition_broadcast(128), w=[b_gb])
        for i in range(NT):
            j = i % 2
            k.dma(xt[j][:], xin[i * 128:(i + 1) * 128, :], r=[b_xin], w=[b_xt[j]])
            k.op(k.ACT, lambda e: e.activation(junk[:], xt[j][:], AF.Square, accum_out=ss[j][:]),
                 r=[b_xt[j]], w=[b_junk, b_ss[j]])
            k.op(k.DVE, lambda e: e.tensor_scalar(rs[j][:], ss[j][:], 1.0 / D, RMS_EPS, ALU.mult, ALU.add),
                 r=[b_ss[j]], w=[b_rs[j]])
            k.op(k.ACT, lambda e: e.sqrt(rs[j][:], rs[j][:]), r=[b_rs[j]], w=[b_rs[j]])
            k.op(k.DVE, lambda e: e.reciprocal(rs[j][:], rs[j][:]), r=[b_rs[j]], w=[b_rs[j]])
            k.op(k.DVE, lambda e: e.scalar_tensor_tensor(yt[j][:], xt[j][:], rs[j][:], gb[:], ALU.mult, ALU.mult),
                 r=[b_xt[j], b_rs[j], b_gb], w=[b_yt[j]])
            k.dma(g.y[i * 128:(i + 1) * 128, :], yt[j][:], r=[b_yt[j]], w=[b_y])


def phase_attn_bias(g):
    nc, k = g.nc, g.k
    b_Gd = g.bufs.setdefault("Gd", Buf("Gd", True))
    k.barrier()
    with ExitStack() as es:
        BK = k.sb(es, "ab_bk", [128, 1152], F32)
        btab = k.sb(es, "ab_bt", [128, 512], F32)
        G = [k.sb(es, f"ab_g{i}", [128, 1152], F32) for i in range(2)]
        tmp = k.sb(es, "ab_tmp", [128, 1152], F32)
        b_BK, b_bt, b_tmp = Buf(), Buf(), Buf()
        b_G = [Buf(), Buf()]
        k.dma(BK[:], g.bkconst, w=[b_BK])
        k.dma(btab[:], g.rel_bias.rearrange("(o b) h -> o (b h)", o=1).partition_broadcast(128), w=[b_bt])
        for h in range(NH_ATT):
            j = h % 2
            for b in range(32):
                sc = btab[:, b * 16 + h:b * 16 + h + 1]
                if b == 0:
                    k.op(k.DVE, lambda e: e.tensor_scalar(G[j][:], BK[:], float(b), sc, ALU.is_equal, ALU.mult),
                         r=[b_BK, b_bt], w=[b_G[j]])
                else:
                    k.op(k.DVE, lambda e: e.tensor_scalar(tmp[:], BK[:], float(b), sc, ALU.is_equal, ALU.mult),
                         r=[b_BK, b_bt], w=[b_tmp])
                    k.op(k.DVE, lambda e: e.tensor_tensor(G[j][:], G[j][:], tmp[:], ALU.add),
                         r=[b_tmp, b_G[j]], w=[b_G[j]])
            k.dma(g.Gd[h], G[j][:], r=[b_G[j]], w=[b_Gd])


def phase_attn(g, l):
    nc, k = g.nc, g.k
    b_qkT = g.bufs.setdefault("qkT", Buf("qkT", True))
    b_vtok = g.bufs.setdefault("vtok", Buf("vtok", True))
    b_Gd = g.bufs.setdefault("Gd", Buf("Gd", True))
    b_mixT = g.bufs.setdefault("mixT", Buf("mixT", True))
    lam_init = 0.8 - 0.6 * math.exp(-0.3 * l)
    scale = 64 ** -0.5
    k.barrier()
    with ExitStack() as es:
        kts = [k.sb(es, f"at_k{i}", [128, T], BF16) for i in range(2)]
        qts = [k.sb(es, f"at_q{i}", [128, T], BF16) for i in range(2)]
        vts = [k.sb(es, f"at_v{i}", [128, NT, 128], BF16) for i in range(2)]
        Gs = [k.sb(es, f"at_g{i}", [128, 1152], F32) for i in range(2)]
        fb = [k.sb(es, f"at_fb{i}", [128, 2, NT], F32) for i in range(2)]
        b_kt = [Buf(), Buf()]
        b_qt = [Buf(), Buf()]
        b_vt = [Buf(), Buf()]
        b_Gs = [Buf(), Buf()]
        b_fb = [Buf(), Buf()]
        btab = k.sb(es, "at_bt", [128, 512], F32)
        kmb = k.sb(es, "at_kmb", [128, NT], F32)
        lp = k.sb(es, "at_lp", [128, 4, 64], F32)
        lt = k.sb(es, "at_lt", [128, 2, 64], F32)
        ld = k.sb(es, "at_ld", [128, 2], F32)
        nlam = k.sb(es, "at_nlam", [128, 1], F32)
        gsc = k.sb(es, "at_gsc", [128, 1], F32)
        ones = k.sb(es, "at_ones", [128, 128], BF16)
        b_bt, b_kmb, b_lp, b_lt, b_ld, b_nlam, b_gsc, b_ones = (Buf() for _ in range(8))
        P = [[k.sb(es, f"at_p{m}{i}", [128, 512], BF16) for i in range(2)] for m in range(2)]
        b_P = [[Buf(), Buf()], [Buf(), Buf()]]
        sbt = [k.sb(es, f"at_sb{i}", [128, 512], F32) for i in range(2)]
        b_sbt = [Buf(), Buf()]
        r0 = k.sb(es, "at_r0", [128, 512], F32)
        r1 = k.sb(es, "at_r1", [128, 512], F32)
        o0 = k.sb(es, "at_o0", [128, 512], F32)
        o1 = k.sb(es, "at_o1", [128, 512], F32)
        sq = k.sb(es, "at_sq", [128, 512], BF16)
        rn = k.sb(es, "at_rn", [128, 512], F32)
        ob = [k.sb(es, f"at_ob{i}", [128, 512], BF16) for i in range(2)]
        b_r0, b_r1, b_o0, b_o1, b_sq, b_rn = (Buf() for _ in range(6))
        b_ob = [Buf(), Buf()]
        psS = [[k.ps(es, f"at_ps{m}{i}", [128, 512]) for i in range(2)] for m in range(2)]
        b_psS = [[Buf(), Buf()], [Buf(), Buf()]]
        psO = [k.ps(es, f"at_po{m}", [128, 512]) for m in range(2)]
        psD = [k.ps(es, f"at_pd{m}", [128, 512]) for m in range(2)]
        b_psO = [Buf(), Buf()]
        b_psD = [Buf(), Buf()]

        k.dma(btab[:], g.rel_bias.rearrange("(o b) h -> o (b h)", o=1).partition_broadcast(128), w=[b_bt])
        k.dma(kmb[:], g.tmask.rearrange("(i p) -> p i", p=128), w=[b_kmb], allow_slow_non_contiguous=True)
        k.op(k.DVE, lambda e: e.tensor_scalar(kmb[:], kmb[:], -1.0, 30000.0, ALU.add, ALU.mult), r=[b_kmb], w=[b_kmb])
        k.dma(lp[:], g.att_lambda[l:l + 1].rearrange("o a d -> o (a d)").partition_broadcast(128), w=[b_lp])
        k.op(k.DVE, lambda e: e.tensor_tensor(lt[:, 0, :], lp[:, 0, :], lp[:, 1, :], ALU.mult), r=[b_lp], w=[b_lt])
        k.op(k.DVE, lambda e: e.tensor_tensor(lt[:, 1, :], lp[:, 2, :], lp[:, 3, :], ALU.mult), r=[b_lp, b_lt], w=[b_lt])
        k.op(k.DVE, lambda e: e.reduce_sum(ld[:], lt[:], AX.X), r=[b_lt], w=[b_ld])
        k.op(k.ACT, lambda e: e.activation(ld[:], ld[:], AF.Exp), r=[b_ld], w=[b_ld])
        k.op(k.DVE, lambda e: e.tensor_tensor(nlam[:], ld[:, 1:2], ld[:, 0:1], ALU.subtract), r=[b_ld], w=[b_nlam])
        k.op(k.DVE, lambda e: e.tensor_scalar_add(nlam[:], nlam[:], -lam_init), r=[b_nlam], w=[b_nlam])
        k.dma(gsc[:], g.att_subln_g[l].rearrange("(p o) -> p o", o=1), w=[b_gsc], allow_slow_non_contiguous=True)
        k.op(k.DVE, lambda e: e.tensor_scalar_mul(gsc[:], gsc[:], 1.0 - lam_init), r=[b_gsc], w=[b_gsc])
        k.op(k.DVE, lambda e: e.memset(ones[:], 1.0), w=[b_ones])

        scnt = 0
        for h in range(NH_ATT):
            j = h % 2
            k.dma(kts[j][:], g.qkT[16 + h], r=[b_qkT], w=[b_kt[j]])
            k.dma(qts[j][:], g.qkT[h], r=[b_qkT], w=[b_qt[j]])
            k.dma(vts[j][:], g.vtok[:, h * 128:(h + 1) * 128].rearrange("(i p) d -> p i d", p=128),
                  r=[b_vtok], w=[b_vt[j]])
            k.dma(Gs[j][:], g.Gd[h], r=[b_Gd], w=[b_Gs[j]])
            k.op(k.DVE, lambda e: e.tensor_scalar(fb[j][:, 0, :], kmb[:], btab[:, 15 * 16 + h:15 * 16 + h + 1], None, ALU.add),
                 r=[b_kmb, b_bt], w=[b_fb[j]])
            k.op(k.DVE, lambda e: e.tensor_scalar(fb[j][:, 1, :], kmb[:], btab[:, 31 * 16 + h:31 * 16 + h + 1], None, ALU.add),
                 r=[b_kmb, b_bt, b_fb[j]], w=[b_fb[j]])
            for qb in range(T // 512):
                for kt in range(NT):
                    delta = kt - 4 * qb
                    near = (-1 <= delta <= 4)
                    si = scnt % 2
                    scnt += 1
                    for m in range(2):
                        k.op(k.PE, lambda e: e.matmul(psS[m][si][:], kts[j][m * 64:(m + 1) * 64, kt * 128:(kt + 1) * 128],
                                                      qts[j][m * 64:(m + 1) * 64, qb * 512:(qb + 1) * 512],
                                                      start=True, stop=True),
                             r=[b_kt[j], b_qt[j]], w=[b_psS[m][si]])
                    for m in range(2):
                        if near:
                            c0 = 512 - delta * 128
                            k.op(k.DVE, lambda e: e.scalar_tensor_tensor(sbt[m][:], psS[m][si][:], scale, Gs[j][:, c0:c0 + 512],
                                                                         ALU.mult, ALU.add),
                                 r=[b_psS[m][si], b_Gs[j]], w=[b_sbt[m]])
                            k.op(k.ACT, lambda e: e.activation(P[m][si][:], sbt[m][:], AF.Exp, bias=kmb[:, kt:kt + 1]),
                                 r=[b_sbt[m], b_kmb], w=[b_P[m][si]])
                        else:
                            fi = 0 if delta < 0 else 1
                            k.op(k.ACT, lambda e: e.activation(P[m][si][:], psS[m][si][:], AF.Exp,
                                                               bias=fb[j][:, fi, kt:kt + 1], scale=scale),
                                 r=[b_psS[m][si], b_fb[j]], w=[b_P[m][si]])
                    for m in range(2):
                        k.op(k.PE, lambda e: e.matmul(psO[m][:], vts[j][:, kt, :], P[m][si][:],
                                                      start=(kt == 0), stop=(kt == NT - 1)),
                             r=[b_vt[j], b_P[m][si]], w=[b_psO[m]], inc=False)
                        k.op(k.PE, lambda e: e.matmul(psD[m][:], ones[:], P[m][si][:],
                                                      start=(kt == 0), stop=(kt == NT - 1)),
                             r=[b_ones, b_P[m][si]], w=[b_psD[m]], inc=True)
                k.op(k.DVE, lambda e: e.reciprocal(r0[:], psD[0][:]), r=[b_psD[0]], w=[b_r0])
                k.op(k.DVE, lambda e: e.reciprocal(r1[:], psD[1][:]), r=[b_psD[1]], w=[b_r1])
                k.op(k.DVE, lambda e: e.tensor_tensor(o0[:], psO[0][:], r0[:], ALU.mult), r=[b_psO[0], b_r0], w=[b_o0])
                k.op(k.DVE, lambda e: e.tensor_tensor(o1[:], psO[1][:], r1[:], ALU.mult), r=[b_psO[1], b_r1], w=[b_o1])
                k.op(k.DVE, lambda e: e.scalar_tensor_tensor(o0[:], o1[:], nlam[:, 0:1], o0[:], ALU.mult, ALU.add),
                     r=[b_o1, b_nlam, b_o0], w=[b_o0])
                k.op(k.ACT, lambda e: e.activation(sq[:], o0[:], AF.Square), r=[b_o0], w=[b_sq])
                pn = psS[0][scnt % 2]
                b_pn = b_psS[0][scnt % 2]
                k.op(k.PE, lambda e: e.matmul(pn[:], ones[:], sq[:], start=True, stop=True), r=[b_ones, b_sq], w=[b_pn])
                k.op(k.DVE, lambda e: e.tensor_scalar(rn[:], pn[:], 1.0 / 128, SUBLN_EPS, ALU.mult, ALU.add), r=[b_pn], w=[b_rn])
                k.op(k.ACT, lambda e: e.sqrt(rn[:], rn[:]), r=[b_rn], w=[b_rn])
                k.op(k.DVE, lambda e: e.reciprocal(rn[:], rn[:]), r=[b_rn], w=[b_rn])
                oj = qb % 2
                k.op(k.DVE, lambda e: e.scalar_tensor_tensor(ob[oj][:], o0[:], gsc[:, 0:1], rn[:], ALU.mult, ALU.mult),
                     r=[b_o0, b_gsc, b_rn], w=[b_ob[oj]])
                k.dma(g.mixT[h, :, qb * 512:(qb + 1) * 512], ob[oj][:], r=[b_ob[oj]], w=[b_mixT])


def _bucket_const():
    kl = np.arange(128)[:, None]
    c = np.arange(1152)[None, :]
    rel = kl - c + 512
    nb = 16
    max_exact = 8
    n = np.abs(rel)
    nf = np.maximum(n, 1).astype(np.float32)
    large = max_exact + (np.log(nf / max_exact) / np.float32(math.log(128 / max_exact)) * (nb - max_exact)).astype(np.int32)
    large = np.minimum(large, nb - 1)
    bk = np.where(rel > 0, nb, 0) + np.where(n < max_exact, n, large)
    return bk.astype(np.float32)


def make_inputs(d, x, c, tm):
    im = {"x": np.ascontiguousarray(x, dtype=np.float32), "c": np.ascontiguousarray(c, dtype=np.float32),
          "tmask": tm, "ident": np.eye(128, dtype=np.float32), "bkconst": _bucket_const()}
    ii = np.arange(128)
    su = (ii[:, None] < ii[None, :]).astype(np.float32)
    iu = (ii[:, None] <= ii[None, :]).astype(np.float32)
    im["trimask"] = np.ascontiguousarray(np.stack([su, iu, su.T, iu.T], axis=1))
    im["onesbd"] = np.kron(np.eye(2, dtype=np.float32), np.ones((64, 64), np.float32))
    for name in ("ada_w", "ada_b", "norm1_g", "w_in", "att_lambda", "att_subln_g", "rel_bias", "w_out", "norm2_g",
                 "ffn_up", "ffn_conv", "ffn_down", "final_g", "rwkv_mu", "rwkv_w0", "rwkv_w_up", "rwkv_a0", "rwkv_a_up",
                 "rwkv_g_up", "rwkv_k_k", "rwkv_k_a", "rwkv_r_k", "rwkv_ln_g", "rwkv_ln_b"):
        im[name] = np.ascontiguousarray(d[name], dtype=np.float32)
    return im


LWS = 0.6065306597126334


class TP:
    def __init__(self, k, es, name, shape, dt, n, psum=False):
        mk = k.ps if psum else k.sb
        self.t = [mk(es, f"{name}{i}", shape, dt) for i in range(n)]
        self.b = [Buf() for _ in range(n)]
        self.i = 0

    def get(self):
        j = self.i % len(self.t)
        self.i += 1
        return self.t[j], self.b[j]


def phase_rwkv_prep(g, l):
    nc, k = g.nc, g.k
    b_zrT = g.bufs.setdefault("zrT", Buf("zrT", True))
    b_ops = g.bufs.setdefault("rk_ops", Buf("rk_ops", True))
    b_gc = g.bufs.setdefault("rk_gc", Buf("rk_gc", True))
    b_vT = g.bufs.setdefault("rk_vT", Buf("rk_vT", True))
    b_g = g.bufs.setdefault("rk_g", Buf("rk_g", True))
    b_bv = g.bufs.setdefault("rk_bv", Buf("rk_bv", True))
    k.barrier()
    W = 512
    with ExitStack() as es:
        def tile(name, shape, dt=F32):
            return k.sb(es, "rp_" + name, shape, dt), Buf()

        mu3, b_mu3 = tile("mu3", [128, 2, 48])
        muc3, b_muc3 = tile("muc3", [128, 48])
        mul, b_mul = tile("mul", [128, 2, 4])
        mucl, b_mucl = tile("mucl", [128, 4])
        w0c, b_w0c = tile("w0c", [128, 2, 16])
        a0c, b_a0c = tile("a0c", [128, 2, 16])
        kkc, b_kkc = tile("kkc", [128, 16])
        kac, b_kac = tile("kac", [128, 16])
        omka, b_omka = tile("omka", [128, 16])
        rkc, b_rkc = tile("rkc", [128, 16])
        wup, b_wup = tile("wup", [128, 2, 2048], BF16)
        aup, b_aup = tile("aup", [128, 2, 2048], BF16)
        gup, b_gup = tile("gup", [128, 2, 2048], BF16)
        onesbd, b_onesbd = tile("onesbd", [128, 128])
        tri, b_tri = tile("tri", [128, 4, 128])
        idf, b_idf = tile("idf", [128, 128])
        zxw, b_zxw = tile("zxw", [128, W + 2])
        zxa, b_zxa = tile("zxa", [128, W + 2])
        zxg, b_zxg = tile("zxg", [128, 2, W + 2])
        lt, b_lt = tile("lt", [128, W])
        tw, b_tw = tile("tw", [128, W], BF16)
        xab, b_xab = tile("xab", [128, W], BF16)
        sgb, b_sgb = tile("sgb", [128, 2, W], BF16)
        zin = [tile(f"zin{i}", [128, W + 2]) for i in range(3)]
        zs = [tile(f"zs{i}", [128, W]) for i in range(3)]
        ld = [tile(f"ld{i}", [128, W]) for i in range(2)]
        aa = [tile(f"aa{i}", [128, W]) for i in range(2)]
        kd = [tile(f"kd{i}", [128, W]) for i in range(2)]
        gf, b_gf = tile("gf", [128, W])
        kk, b_kk = tile("kk", [128, W])
        sq, b_sq = tile("sq", [128, W])
        rn, b_rn = tile("rn", [128, W])
        tk, b_tk = tile("tk", [128, W])
        bt, b_bt = tile("bt", [128, W])
        bv, b_bv_t = tile("bv", [128, W])
        vb, b_vb = tile("vb", [128, W], BF16)
        ldT, b_ldT = tile("ldT", [128, W])
        cum, b_cum = tile("cum", [128, W])
        ecp, b_ecp = tile("ecp", [128, W])
        ecm, b_ecm = tile("ecm", [128, W])
        cex, b_cex = tile("cex", [128, W])
        ect, b_ect = tile("ect", [128, W])
        ba, b_ba = tile("ba", [128, W])
        tot, b_tot = tile("tot", [128, 4])
        gcv, b_gcv = tile("gcv", [128, 4])
        outs = TP(k, es, "rp_out", [128, W], BF16, 6)
        tmb, b_tmb = tile("tmb", [128, T])
        k.dma(tmb[:], g.tmask.rearrange("(o t) -> o t", o=1).partition_broadcast(128), w=[b_tmb])
        PS = TP(k, es, "rp_ps", [128, 512], F32, 8, psum=True)

        k.op(k.DVE, lambda e: e.memset(mul[:], 0.0), w=[b_mul])
        for d in range(2):
            k.dma(mu3[:, d, :], g.rwkv_mu[l, d, 0:6144].rearrange("(j p) -> p j", p=128), w=[b_mu3],
                  allow_slow_non_contiguous=True)
            k.dma(mul[0:96, d, 0:1], g.rwkv_mu[l, d, 6144:6240].rearrange("(p o) -> p o", o=1), w=[b_mul],
                  allow_slow_non_contiguous=True)
            k.dma(mul[0:96, d, 1:2], g.rwkv_mu[l, d, 6240:6336].rearrange("(p o) -> p o", o=1), w=[b_mul],
                  allow_slow_non_contiguous=True)
            k.dma(mul[:, d, 2:4], g.rwkv_mu[l, d, 6336:6592].rearrange("(j p) -> p j", p=128), w=[b_mul],
                  allow_slow_non_contiguous=True)
            k.dma(w0c[:, d, :], g.rwkv_w0[l, d].rearrange("(j p) -> p j", p=128), w=[b_w0c], allow_slow_non_contiguous=True)
            k.dma(a0c[:, d, :], g.rwkv_a0[l, d].rearrange("(j p) -> p j", p=128), w=[b_a0c], allow_slow_non_contiguous=True)
            k.dma(wup[0:96, d, :], g.rwkv_w_up[l, d], w=[b_wup], q=k.POOL)
            k.dma(aup[0:96, d, :], g.rwkv_a_up[l, d], w=[b_aup], q=k.POOL)
            k.dma(gup[:, d, :], g.rwkv_g_up[l, d * 128:(d + 1) * 128, :], w=[b_gup], q=k.POOL)
        k.dma(kkc[:], g.rwkv_k_k[l].rearrange("(j p) -> p j", p=128), w=[b_kkc], allow_slow_non_contiguous=True)
        k.dma(kac[:], g.rwkv_k_a[l].rearrange("(j p) -> p j", p=128), w=[b_kac], allow_slow_non_contiguous=True)
        k.dma(rkc[:], g.rwkv_r_k[l].rearrange("h n -> (h n)").rearrange("(j p) -> p j", p=128), w=[b_rkc],
              allow_slow_non_contiguous=True)
        k.dma(onesbd[:], g.onesbd, w=[b_onesbd])
        k.dma(tri[:], g.trimask, w=[b_tri])
        k.dma(idf[:], g.ident, w=[b_idf])
        k.op(k.DVE, lambda e: e.tensor_tensor(muc3[:], mu3[:, 0, :], mu3[:, 1, :], ALU.add), r=[b_mu3], w=[b_muc3])
        k.op(k.DVE, lambda e: e.tensor_scalar(muc3[:], muc3[:], -1.0, 1.0, ALU.mult, ALU.add), r=[b_muc3], w=[b_muc3])
        k.op(k.DVE, lambda e: e.tensor_tensor(mucl[:], mul[:, 0, :], mul[:, 1, :], ALU.add), r=[b_mul], w=[b_mucl])
        k.op(k.DVE, lambda e: e.tensor_scalar(mucl[:], mucl[:], -1.0, 1.0, ALU.mult, ALU.add), r=[b_mucl], w=[b_mucl])
        k.op(k.DVE, lambda e: e.tensor_scalar(omka[:], kac[:], -1.0, 1.0, ALU.mult, ALU.add), r=[b_kac], w=[b_omka])

        def shiftmix(dst, b_dst, src, b_src, mc, m0, m1, bm, npart):
            k.op(k.ACT, lambda e: e.activation(dst[0:npart, :], src[0:npart, 1:W + 1], AF.Copy, scale=mc[0:npart]),
                 r=[b_src] + bm, w=[b_dst])
            k.op(k.DVE, lambda e: e.scalar_tensor_tensor(dst[0:npart, :], src[0:npart, 0:W], m0[0:npart], dst[0:npart, :],
                                                         ALU.mult, ALU.add), r=[b_src, b_dst] + bm, w=[b_dst])
            k.op(k.DVE, lambda e: e.scalar_tensor_tensor(dst[0:npart, :], src[0:npart, 2:W + 2], m1[0:npart], dst[0:npart, :],
                                                         ALU.mult, ALU.add), r=[b_src, b_dst] + bm, w=[b_dst])

        bml = [b_mul, b_mucl]
        bm3 = [b_mu3, b_muc3]
        for tb in range(T // W):
            t0 = tb * W
            k.dma(zxw[0:96, :], g.zrT[6144:6240, t0:t0 + W + 2], r=[b_zrT], w=[b_zxw])
            k.dma(zxa[0:96, :], g.zrT[6240:6336, t0:t0 + W + 2], r=[b_zrT], w=[b_zxa])
            k.dma(zxg[:], g.zrT[6336:6592, t0:t0 + W + 2].rearrange("(j p) t -> p j t", p=128), r=[b_zrT], w=[b_zxg])
            shiftmix(lt, b_lt, zxw, b_zxw, mucl[:, 0:1], mul[:, 0, 0:1], mul[:, 1, 0:1], bml, 96)
            k.op(k.ACT, lambda e: e.activation(tw[0:96, :], lt[0:96, :], AF.Tanh), r=[b_lt], w=[b_tw])
            shiftmix(lt, b_lt, zxa, b_zxa, mucl[:, 1:2], mul[:, 0, 1:2], mul[:, 1, 1:2], bml, 96)
            k.op(k.ACT, lambda e: e.copy(xab[0:96, :], lt[0:96, :]), r=[b_lt], w=[b_xab])
            for jj in range(2):
                shiftmix(lt, b_lt, zxg[:, jj, :], b_zxg, mucl[:, 2 + jj:3 + jj], mul[:, 0, 2 + jj:3 + jj],
                         mul[:, 1, 2 + jj:3 + jj], bml, 128)
                k.op(k.ACT, lambda e: e.activation(sgb[:, jj, :], lt[:], AF.Sigmoid), r=[b_lt], w=[b_sgb])
            for hp in range(16):
                cs = slice(hp * 128, (hp + 1) * 128)
                for i3 in range(3):
                    row0 = i3 * 2048 + hp * 128
                    k.dma(zin[i3][0][:], g.zrT[row0:row0 + 128, t0:t0 + W + 2], r=[b_zrT], w=[zin[i3][1]])
                    col = i3 * 16 + hp
                    shiftmix(zs[i3][0], zs[i3][1], zin[i3][0], zin[i3][1], muc3[:, col:col + 1], mu3[:, 0, col:col + 1],
                             mu3[:, 1, col:col + 1], bm3, 128)
                (r_s, b_rs), (k_s, b_ks), (v_s, b_vs) = zs
                k.op(k.DVE, lambda e: e.tensor_tensor(v_s[:], v_s[:], tmb[:, t0:t0 + W], ALU.mult), r=[b_vs, b_tmb], w=[b_vs])
                k.op(k.ACT, lambda e: e.copy(vb[:], v_s[:]), r=[b_vs], w=[b_vb])
                k.dma(g.rk_vT[hp, :, t0:t0 + W], vb[:], r=[b_vb], w=[b_vT])
                for d in range(2):
                    pw, b_pw = PS.get()
                    k.op(k.PE, lambda e: e.matmul(pw[:], wup[0:96, d, cs], tw[0:96, :], start=True, stop=True),
                         r=[b_wup, b_tw], w=[b_pw])
                    k.op(k.ACT, lambda e: e.activation(ld[d][0][:], pw[:], AF.Sigmoid, bias=w0c[:, d, hp:hp + 1]),
                         r=[b_pw, b_w0c], w=[ld[d][1]])
                    pa, b_pa = PS.get()
                    k.op(k.PE, lambda e: e.matmul(pa[:], aup[0:96, d, cs], xab[0:96, :], start=True, stop=True),
                         r=[b_aup, b_xab], w=[b_pa])
                    k.op(k.ACT, lambda e: e.activation(aa[d][0][:], pa[:], AF.Sigmoid, bias=a0c[:, d, hp:hp + 1]),
                         r=[b_pa, b_a0c], w=[aa[d][1]])
                pg, b_pg = PS.get()
                k.op(k.PE, lambda e: e.matmul(pg[:], gup[:, 0, cs], sgb[:, 0, :], start=True, stop=False),
                     r=[b_gup, b_sgb], w=[b_pg], inc=False)
                k.op(k.PE, lambda e: e.matmul(pg[:], gup[:, 1, cs], sgb[:, 1, :], start=False, stop=True),
                     r=[b_gup, b_sgb], w=[b_pg])
                k.op(k.ACT, lambda e: e.copy(gf[:], pg[:]), r=[b_pg], w=[b_gf])
                k.dma(g.rk_g[hp, :, t0:t0 + W], gf[:], r=[b_gf], w=[b_g])
                k.op(k.DVE, lambda e: e.tensor_scalar(kk[:], k_s[:], kkc[:, hp:hp + 1], None, ALU.mult), r=[b_ks, b_kkc], w=[b_kk])
                k.op(k.DVE, lambda e: e.tensor_tensor(sq[:], kk[:], kk[:], ALU.mult), r=[b_kk], w=[b_sq])
                pss, b_pss = PS.get()
                k.op(k.PE, lambda e: e.matmul(pss[:], onesbd[:], sq[:], start=True, stop=True), r=[b_onesbd, b_sq], w=[b_pss])
                k.op(k.ACT, lambda e: e.sqrt(rn[:], pss[:]), r=[b_pss], w=[b_rn])
                k.op(k.DVE, lambda e: e.tensor_scalar_max(rn[:], rn[:], 1e-12), r=[b_rn], w=[b_rn])
                k.op(k.DVE, lambda e: e.reciprocal(rn[:], rn[:]), r=[b_rn], w=[b_rn])
                k.op(k.DVE, lambda e: e.tensor_tensor(kk[:], kk[:], rn[:], ALU.mult), r=[b_kk, b_rn], w=[b_kk])
                for d in range(2):
                    k.op(k.DVE, lambda e: e.tensor_scalar(tk[:], aa[d][0][:], kac[:, hp:hp + 1], omka[:, hp:hp + 1], ALU.mult, ALU.add),
                         r=[aa[d][1], b_kac, b_omka], w=[b_tk])
                    k.op(k.DVE, lambda e: e.tensor_tensor(kd[d][0][:], k_s[:], tk[:], ALU.mult), r=[b_ks, b_tk], w=[kd[d][1]])
                k.op(k.DVE, lambda e: e.tensor_tensor(bt[:], kd[0][0][:], kd[1][0][:], ALU.add), r=[kd[0][1], kd[1][1]], w=[b_bt])
                k.op(k.DVE, lambda e: e.tensor_tensor(bt[:], bt[:], r_s[:], ALU.mult), r=[b_bt, b_rs], w=[b_bt])
                k.op(k.DVE, lambda e: e.tensor_scalar(bt[:], bt[:], rkc[:, hp:hp + 1], None, ALU.mult), r=[b_bt, b_rkc], w=[b_bt])
                pb, b_pb = PS.get()
                k.op(k.PE, lambda e: e.matmul(pb[:], onesbd[:], bt[:], start=True, stop=True), r=[b_onesbd, b_bt], w=[b_pb])
                k.op(k.DVE, lambda e: e.tensor_tensor(bv[:], pb[:], v_s[:], ALU.mult), r=[b_pb, b_vs], w=[b_bv_t])
                k.dma(g.rk_bv[hp, :, t0:t0 + W], bv[:], r=[b_bv_t], w=[b_bv])
                for d in range(2):
                    ldd, b_ldd = ld[d]
                    pT, b_pT = PS.get()
                    for ci in range(4):
                        k.op(k.PE, lambda e: e.transpose(pT[:, ci * 128:(ci + 1) * 128], ldd[:, ci * 128:(ci + 1) * 128], idf[:]),
                             r=[b_ldd, b_idf], w=[b_pT], inc=(ci == 3))
                    k.op(k.DVE, lambda e: e.tensor_copy(ldT[:], pT[:]), r=[b_pT], w=[b_ldT])
                    pc, b_pc = PS.get()
                    for ci in range(4):
                        k.op(k.PE, lambda e: e.matmul(pc[:, ci * 128:(ci + 1) * 128], ldT[:, ci * 128:(ci + 1) * 128],
                                                      tri[:, 1 if d == 0 else 3, :], start=True, stop=True),
                             r=[b_ldT, b_tri], w=[b_pc], inc=(ci == 3))
                    k.op(k.ACT, lambda e: e.mul(cum[:], pc[:], -LWS), r=[b_pc], w=[b_cum])
                    k.op(k.ACT, lambda e: e.activation(ecp[:], cum[:], AF.Exp), r=[b_cum], w=[b_ecp])
                    k.op(k.ACT, lambda e: e.activation(ecm[:], cum[:], AF.Exp, scale=-1.0), r=[b_cum], w=[b_ecm])
                    k.op(k.DVE, lambda e: e.scalar_tensor_tensor(cex[:], ldd[:], LWS, cum[:], ALU.mult, ALU.add),
                         r=[b_ldd, b_cum], w=[b_cex])
                    k.op(k.ACT, lambda e: e.activation(cex[:], cex[:], AF.Exp), r=[b_cex], w=[b_cex])
                    c3 = cum[:].rearrange("p (c t) -> p c t", c=4)
                    endcol = 127 if d == 0 else 0
                    k.op(k.DVE, lambda e: e.tensor_copy(tot[:], c3[:, :, endcol]), r=[b_cum], w=[b_tot])
                    k.op(k.ACT, lambda e: e.activation(gcv[:], tot[:], AF.Exp), r=[b_tot], w=[b_gcv])
                    k.dma(g.rk_gc[d, hp, :, tb * 4:(tb + 1) * 4], gcv[:], r=[b_gcv], w=[b_gc])
                    for ci in range(4):
                        k.op(k.ACT, lambda e: e.activation(ect[:, ci * 128:(ci + 1) * 128], cum[:, ci * 128:(ci + 1) * 128], AF.Exp,
                                                           bias=tot[:, ci:ci + 1], scale=-1.0),
                             r=[b_cum, b_tot], w=[b_ect])
                    k.op(k.DVE, lambda e: e.tensor_tensor(ba[:], kk[:], aa[d][0][:], ALU.mult), r=[b_kk, aa[d][1]], w=[b_ba])
                    prods = [
                        (0, lambda e, o: e.scalar_tensor_tensor(o[:], kk[:], -1.0, cex[:], ALU.mult, ALU.mult), [b_kk, b_cex]),
                        (1, lambda e, o: e.tensor_tensor(o[:], r_s[:], ecp[:], ALU.mult), [b_rs, b_ecp]),
                        (2, lambda e, o: e.tensor_tensor(o[:], ba[:], ecm[:], ALU.mult), [b_ba, b_ecm]),
                        (3, lambda e, o: e.tensor_tensor(o[:], kd[d][0][:], ecm[:], ALU.mult), [kd[d][1], b_ecm]),
                        (4, lambda e, o: e.tensor_tensor(o[:], ba[:], ect[:], ALU.mult), [b_ba, b_ect]),
                        (5, lambda e, o: e.tensor_tensor(o[:], kd[d][0][:], ect[:], ALU.mult), [kd[d][1], b_ect]),
                    ]
                    for (oi, fn, rr) in prods:
                        o, b_o = outs.get()
                        k.op(k.DVE, lambda e: fn(e, o), r=rr, w=[b_o])
                        k.dma(g.rk_ops[d, oi, hp, :, t0:t0 + W], o[:], r=[b_o], w=[b_ops])


def phase_rwkv_scan(g, l):
    nc, k = g.nc, g.k
    b_ops = g.bufs.setdefault("rk_ops", Buf("rk_ops", True))
    b_gc = g.bufs.setdefault("rk_gc", Buf("rk_gc", True))
    b_vT = g.bufs.setdefault("rk_vT", Buf("rk_vT", True))
    b_y = g.bufs.setdefault("rk_y", Buf("rk_y", True))
    k.barrier()
    NCH = T // 128
    with ExitStack() as es:
        def tile(name, shape, dt=F32):
            return k.sb(es, "rs_" + name, shape, dt), Buf()

        tri, b_tri = tile("tri", [128, 4, 128])
        idf, b_idf = tile("idf", [128, 128])
        idb, b_idb = tile("idb", [128, 128], BF16)
        gcs, b_gcs = tile("gcs", [128, 2, 16, NCH])
        S = [[tile(f"S{d}_{p}", [128, 128]) for p in range(16)] for d in range(2)]
        Sb = [[tile(f"Sb{d}_{p}", [128, 128], BF16) for p in range(16)] for d in range(2)]
        OPS = TP(k, es, "rs_ops", [128, 6, 128], BF16, 3)
        VTp = TP(k, es, "rs_vT", [128, 128], BF16, 3)
        TOK = TP(k, es, "rs_tok", [128, 3, 128], BF16, 2)
        A1 = TP(k, es, "rs_a1", [128, 256], BF16, 4)
        MN = TP(k, es, "rs_mn", [128, 128], F32, 8)
        M32P = TP(k, es, "rs_m32", [128, 128], F32, 4)
        PP = TP(k, es, "rs_pp", [128, 128], BF16, 6)
        PP32 = TP(k, es, "rs_pp32", [128, 128], F32, 4)
        NNP = TP(k, es, "rs_nn", [128, 128], F32, 4)
        XU = TP(k, es, "rs_xu", [128, 128], BF16, 4)
        YO = TP(k, es, "rs_yo", [128, 128], F32, 3)
        PS = TP(k, es, "rs_ps", [128, 512], F32, 6, psum=True)
        PSB = TP(k, es, "rs_psb", [128, 1024], BF16, 2, psum=True)

        k.dma(tri[:], g.trimask, w=[b_tri])
        k.dma(idf[:], g.ident, w=[b_idf])
        k.op(k.DVE, lambda e: e.tensor_copy(idb[:], idf[:]), r=[b_idf], w=[b_idb])
        k.dma(gcs[:], g.rk_gc.rearrange("d j p c -> p d j c"), r=[b_gc], w=[b_gcs])
        for d in range(2):
            for p in range(16):
                k.op(k.DVE, lambda e: e.memset(S[d][p][0][:], 0.0), w=[S[d][p][1]])
                k.op(k.DVE, lambda e: e.memset(Sb[d][p][0][:], 0.0), w=[Sb[d][p][1]])

        ecnt = [0]

        def evac(dst, b_dst, src, b_src, extra_r=()):
            ecnt[0] += 1
            if ecnt[0] % 2 == 0:
                k.op(k.ACT, lambda e: e.copy(dst, src), r=[b_src] + list(extra_r), w=[b_dst])
            else:
                k.op(k.DVE, lambda e: e.tensor_copy(dst, src), r=[b_src] + list(extra_r), w=[b_dst])

        for step in range(NCH if g.scan_steps is None else g.scan_steps):
            for d in range(2):
                n = step if d == 0 else NCH - 1 - step
                ts = slice(n * 128, (n + 1) * 128)
                m_strict = 0 if d == 0 else 2
                m_incl = 1 if d == 0 else 3
                m_nn = 2 if d == 0 else 0
                for p in range(16 if g.scan_cut >= 99 else 1):
                    ops, b_o = OPS.get()
                    k.dma(ops[:], g.rk_ops[d, :, p, :, ts].rearrange("i p t -> p i t"), r=[b_ops], w=[b_o])
                    vT, b_v = VTp.get()
                    k.dma(vT[:], g.rk_vT[p, :, ts], r=[b_vT], w=[b_v])
                    aT, rT, bT, kT, btT, ktT = (ops[:, i, :] for i in range(6))
                    St, b_S = S[d][p]
                    Sbt, b_Sb = Sb[d][p]
                    ptr_b, b_ptr = PSB.get()
                    for i3, src in enumerate((vT[:], btT, ktT)):
                        k.op(k.PE, lambda e: e.transpose(ptr_b[:, i3 * 128:(i3 + 1) * 128], src, idb[:]),
                             r=[b_v, b_o, b_idb], w=[b_ptr], inc=(i3 == 2))
                    tok, b_tok = TOK.get()
                    evac(tok[:].rearrange("p a b -> p (a b)"), b_tok, ptr_b[:, 0:384], b_ptr)
                    Vt, Bt, Kt = tok[:, 0, :], tok[:, 1, :], tok[:, 2, :]
                    if g.scan_cut <= 1:
                        continue
                    KA, BA, NN, MM = [], [], [], []
                    for h in range(2):
                        hs = slice(h * 64, (h + 1) * 64)
                        pk, b_pk = PS.get()
                        k.op(k.PE, lambda e: e.matmul(pk[:, 0:256], kT[hs, :], ops[hs, 0:2, :].rearrange("p a b -> p (a b)"),
                                                      start=True, stop=True), r=[b_o], w=[b_pk])
                        ka, b_ka = A1.get()
                        k.op(k.DVE, lambda e: e.tensor_tensor(ka[:, 0:128], pk[:, 0:128], tri[:, m_strict, :], ALU.mult),
                             r=[b_pk, b_tri], w=[b_ka])
                        k.op(k.DVE, lambda e: e.tensor_tensor(ka[:, 128:256], pk[:, 128:256], tri[:, m_incl, :], ALU.mult),
                             r=[b_pk, b_tri, b_ka], w=[b_ka])
                        pb, b_pb = PS.get()
                        k.op(k.PE, lambda e: e.matmul(pb[:, 0:256], bT[hs, :], ops[hs, 0:2, :].rearrange("p a b -> p (a b)"),
                                                      start=True, stop=True), r=[b_o], w=[b_pb])
                        bb, b_bb = A1.get()
                        m32, b_m32 = M32P.get()
                        k.op(k.DVE, lambda e: e.tensor_tensor(m32[:], pb[:, 0:128], tri[:, m_strict, :], ALU.mult),
                             r=[b_pb, b_tri], w=[b_m32])
                        k.op(k.DVE, lambda e: e.tensor_tensor(bb[:, 128:256], pb[:, 128:256], tri[:, m_incl, :], ALU.mult),
                             r=[b_pb, b_tri], w=[b_bb])
                        pn, b_pn = PS.get()
                        k.op(k.PE, lambda e: e.matmul(pn[:, 0:128], aT[hs, :], bT[hs, :], start=True, stop=True), r=[b_o], w=[b_pn])
                        nn, b_nn = NNP.get()
                        k.op(k.DVE, lambda e: e.tensor_tensor(nn[:], pn[:, 0:128], tri[:, m_nn, :], ALU.mult),
                             r=[b_pn, b_tri], w=[b_nn])
                        KA.append((ka, b_ka))
                        BA.append((bb, b_bb))
                        MM.append((m32, b_m32))
                        NN.append((nn, b_nn))
                    if g.scan_cut <= 2:
                        continue
                    Pm = []
                    for h in range(2):
                        Mc, b_Mc = MM[h][0][:], MM[h][1]
                        Nc, b_Nc = NN[h][0][:], NN[h][1]
                        P32, b_P32 = PP32.get()
                        k.op(k.DVE, lambda e: e.tensor_tensor(P32[:], Mc, idf[:], ALU.add), r=[b_Mc, b_idf], w=[b_P32])
                        for lev in range(1, 7):
                            pmA, b_pmA = PS.get()
                            k.op(k.PE, lambda e: e.matmul(pmA[:, 0:128], Nc, Mc, start=True, stop=True), r=[b_Nc, b_Mc], w=[b_pmA])
                            pmB, b_pmB = PS.get()
                            k.op(k.PE, lambda e: e.matmul(pmB[:, 0:128], Mc, Nc, start=True, stop=True), r=[b_Nc, b_Mc], w=[b_pmB])
                            m2, b_m2 = MN.get()
                            n2, b_n2 = MN.get()
                            k.op(k.ACT, lambda e: e.copy(m2[:], pmA[:, 0:128]), r=[b_pmA], w=[b_m2])
                            k.op(k.DVE, lambda e: e.tensor_copy(n2[:], pmB[:, 0:128]), r=[b_pmB], w=[b_n2])
                            Mc, b_Mc, Nc, b_Nc = m2[:], b_m2, n2[:], b_n2
                            pq, b_pq = PS.get()
                            k.op(k.PE, lambda e: e.matmul(pq[:, 0:128], Nc, P32[:], start=True, stop=True), r=[b_Nc, b_P32], w=[b_pq])
                            P32n, b_P32n = PP32.get()
                            k.op(k.DVE, lambda e: e.tensor_tensor(P32n[:], pq[:, 0:128], P32[:], ALU.add), r=[b_pq, b_P32], w=[b_P32n])
                            P32, b_P32 = P32n, b_P32n
                        P, b_P = PP.get()
                        k.op(k.ACT, lambda e: e.copy(P[:], P32[:]), r=[b_P32], w=[b_P])
                        Pm.append((P, b_P))
                    if g.scan_cut <= 3:
                        continue
                    px, b_px = PS.get()
                    k.op(k.PE, lambda e: e.matmul(px[:, 0:128], aT, Sbt[:], start=True, stop=False), r=[b_o, b_Sb], w=[b_px], inc=False)
                    for h in range(2):
                        hc = slice(h * 64, (h + 1) * 64)
                        k.op(k.PE, lambda e: e.matmul(px[:, hc], KA[h][0][:, 0:128], Vt[:, hc], start=False, stop=(h == 1)),
                             r=[KA[h][1], b_tok], w=[b_px], inc=(h == 1))
                    X, b_X = XU.get()
                    evac(X[:], b_X, px[:, 0:128], b_px)
                    pu, b_pu = PS.get()
                    for h in range(2):
                        hc = slice(h * 64, (h + 1) * 64)
                        k.op(k.PE, lambda e: e.matmul(pu[:, hc], Pm[h][0][:], X[:, hc], start=True, stop=True),
                             r=[Pm[h][1], b_X], w=[b_pu], inc=(h == 1))
                    U, b_U = XU.get()
                    evac(U[:], b_U, pu[:, 0:128], b_pu)
                    if g.scan_cut <= 4:
                        continue
                    py, b_py = PS.get()
                    k.op(k.PE, lambda e: e.matmul(py[:, 0:128], rT, Sbt[:], start=True, stop=False), r=[b_o, b_Sb], w=[b_py], inc=False)
                    for h in range(2):
                        hc = slice(h * 64, (h + 1) * 64)
                        k.op(k.PE, lambda e: e.matmul(py[:, hc], BA[h][0][:, 128:256], U[:, hc], start=False, stop=False),
                             r=[BA[h][1], b_U], w=[b_py], inc=False)
                        k.op(k.PE, lambda e: e.matmul(py[:, hc], KA[h][0][:, 128:256], Vt[:, hc], start=False, stop=(h == 1)),
                             r=[KA[h][1], b_tok], w=[b_py], inc=(h == 1))
                    yo, b_yo = YO.get()
                    evac(yo[:], b_yo, py[:, 0:128], b_py)
                    k.dma(g.rk_y[d, ts, p * 128:(p + 1) * 128], yo[:], r=[b_yo], w=[b_y])
                    if g.scan_cut <= 5:
                        continue
                    psn, b_psn = PS.get()
                    k.op(k.PE, lambda e: e.matmul(psn[:, 0:128], Bt, U[:], start=True, stop=False), r=[b_tok, b_U], w=[b_psn], inc=False)
                    k.op(k.PE, lambda e: e.matmul(psn[:, 0:128], Kt, Vt, start=False, stop=True), r=[b_tok], w=[b_psn])
                    for h in range(2):
                        hs = slice(h * 64, (h + 1) * 64)
                        k.op(k.DVE, lambda e: e.scalar_tensor_tensor(St[hs, hs], St[hs, hs], gcs[hs, d, p, n:n + 1], psn[hs, hs],
                                                                     ALU.mult, ALU.add),
                             r=[b_S, b_gcs, b_psn], w=[b_S])
                    k.op(k.ACT, lambda e: e.copy(Sbt[:], St[:]), r=[b_S], w=[b_Sb])


def phase_rwkv_post(g, l):
    nc, k = g.nc, g.k
    b_y = g.bufs.setdefault("rk_y", Buf("rk_y", True))
    b_g = g.bufs.setdefault("rk_g", Buf("rk_g", True))
    b_bv = g.bufs.setdefault("rk_bv", Buf("rk_bv", True))
    b_mixT = g.bufs.setdefault("mixT", Buf("mixT", True))
    k.barrier()
    with ExitStack() as es:
        def tile(name, shape, dt=F32):
            return k.sb(es, "rq_" + name, shape, dt), Buf()

        idf, b_idf = tile("idf", [128, 128])
        lng, b_lng = tile("lng", [128, 16])
        lnb, b_lnb = tile("lnb", [128, 16])
        y0 = [tile(f"y0{i}", [128, 2048]) for i in range(2)]
        y1 = [tile(f"y1{i}", [128, 2048]) for i in range(2)]
        sqv, b_sqv = tile("sqv", [128, 2048])
        s1, b_s1 = tile("s1", [128, 32])
        s2, b_s2 = tile("s2", [128, 32])
        gt = [tile(f"g{i}", [128, 16, 128]) for i in range(2)]
        bvt = [tile(f"bv{i}", [128, 16, 128]) for i in range(2)]
        yT, b_yT = tile("yT", [128, 16, 128])
        ob = [tile(f"ob{i}", [128, 16, 128], BF16) for i in range(2)]
        PS = TP(k, es, "rq_ps", [128, 512], F32, 4, psum=True)
        k.dma(idf[:], g.ident, w=[b_idf])
        k.dma(lng[:], g.rwkv_ln_g[l].rearrange("(j p) -> p j", p=128), w=[b_lng], allow_slow_non_contiguous=True)
        k.dma(lnb[:], g.rwkv_ln_b[l].rearrange("(j p) -> p j", p=128), w=[b_lnb], allow_slow_non_contiguous=True)
        for i in range(NT):
            j = i % 2
            ts = slice(i * 128, (i + 1) * 128)
            ya, b_ya = y0[j]
            yb, b_yb = y1[j]
            k.dma(ya[:], g.rk_y[0, ts, :], r=[b_y], w=[b_ya])
            k.dma(yb[:], g.rk_y[1, ts, :], r=[b_y], w=[b_yb])
            k.dma(gt[j][0][:], g.rk_g[:, :, ts].rearrange("j p t -> p j t"), r=[b_g], w=[gt[j][1]])
            k.dma(bvt[j][0][:], g.rk_bv[:, :, ts].rearrange("j p t -> p j t"), r=[b_bv], w=[bvt[j][1]])
            k.op(k.DVE, lambda e: e.tensor_tensor(ya[:], ya[:], yb[:], ALU.add), r=[b_ya, b_yb], w=[b_ya])
            y3 = ya[:].rearrange("p (h n) -> p h n", n=64)
            k.op(k.DVE, lambda e: e.reduce_sum(s1[:], y3, AX.X), r=[b_ya], w=[b_s1])
            k.op(k.DVE, lambda e: e.tensor_scalar_mul(s1[:], s1[:], 1.0 / 64), r=[b_s1], w=[b_s1])
            k.op(k.DVE, lambda e: e.tensor_tensor(y3, y3, s1[:].unsqueeze(2).to_broadcast([128, 32, 64]), ALU.subtract),
                 r=[b_ya, b_s1], w=[b_ya])
            k.op(k.ACT, lambda e: e.activation(sqv[:], ya[:], AF.Square), r=[b_ya], w=[b_sqv])
            k.op(k.DVE, lambda e: e.reduce_sum(s2[:], sqv[:].rearrange("p (h n) -> p h n", n=64), AX.X), r=[b_sqv], w=[b_s2])
            k.op(k.DVE, lambda e: e.tensor_scalar(s2[:], s2[:], 1.0 / 64, GN_EPS, ALU.mult, ALU.add), r=[b_s2], w=[b_s2])
            k.op(k.ACT, lambda e: e.sqrt(s2[:], s2[:]), r=[b_s2], w=[b_s2])
            k.op(k.DVE, lambda e: e.reciprocal(s2[:], s2[:]), r=[b_s2], w=[b_s2])
            k.op(k.DVE, lambda e: e.tensor_tensor(y3, y3, s2[:].unsqueeze(2).to_broadcast([128, 32, 64]), ALU.mult),
                 r=[b_ya, b_s2], w=[b_ya])
            for q in range(4):
                pt, b_pt = PS.get()
                for jj in range(4):
                    p = q * 4 + jj
                    k.op(k.PE, lambda e: e.transpose(pt[:, jj * 128:(jj + 1) * 128], ya[:, p * 128:(p + 1) * 128], idf[:]),
                         r=[b_ya, b_idf], w=[b_pt], inc=(jj == 3))
                for jj in range(4):
                    p = q * 4 + jj
                    k.op(k.ACT, lambda e: e.activation(yT[:, p, :], pt[:, jj * 128:(jj + 1) * 128], AF.Identity,
                                                       bias=lnb[:, p:p + 1], scale=lng[:, p:p + 1]),
                         r=[b_pt, b_lng, b_lnb], w=[b_yT])
            k.op(k.DVE, lambda e: e.tensor_tensor(yT[:], yT[:], bvt[j][0][:], ALU.add), r=[b_yT, bvt[j][1]], w=[b_yT])
            k.op(k.DVE, lambda e: e.tensor_tensor(ob[j][0][:], yT[:], gt[j][0][:], ALU.mult), r=[b_yT, gt[j][1]], w=[ob[j][1]])
            k.dma(g.mixT[16:32, :, ts].rearrange("j p t -> p j t"), ob[j][0][:], r=[ob[j][1]], w=[b_mixT])


def phase_rwkv(g, l):
    phase_rwkv_prep(g, l)
    phase_rwkv_scan(g, l)
    phase_rwkv_post(g, l)


def kernel(**inputs):
    nc = build_program()
    in_maps = []
    xs = np.asarray(inputs["x_sample"], dtype=np.float32)
    xp = np.asarray(inputs["x_prompt"], dtype=np.float32)
    ns = xs.shape[1]
    for core in range(8):
        if core < 4:
            x = xp[core]
            c = np.asarray(inputs["c_prompt"], dtype=np.float32)[core]
            tm = np.ones(T, np.float32)
        else:
            x = np.zeros((T, D), np.float32)
            x[:ns] = xs[core - 4]
            c = np.asarray(inputs["c_sample"], dtype=np.float32)[core - 4]
            tm = np.zeros(T, np.float32)
            tm[:ns] = 1.0
        in_maps.append(make_inputs(inputs, x, c, tm))
    res = run_bass_kernel_spmd(nc, in_maps, core_ids=list(range(8)))
    y_prompt = np.stack([np.asarray(res.results[i]["y"], dtype=np.float32) for i in range(4)])
    y_sample = np.stack([np.asarray(res.results[4 + i]["y"], dtype=np.float32)[:ns] for i in range(4)])
    return (y_prompt, y_sample)
```

```python
import math
from contextlib import ExitStack
import numpy as np
import concourse.bass as bass
import concourse.mybir as mybir
from concourse.bass_utils import run_bass_kernel_spmd

F32 = mybir.dt.float32
BF16 = mybir.dt.bfloat16
ALU = mybir.AluOpType
AF = mybir.ActivationFunctionType
AX = mybir.AxisListType

D = 4096
T = 4096
NL = 2
KC = D // 128
C_ATT = 2048
C_RWKV = 2048
NH_ATT = 16
NH_RWKV = 32
R_W = 96
R_A = 96
R_G = 256
RWKV_COLS = 3 * C_RWKV + R_W + R_A + R_G
IN_COLS = 3 * C_ATT + RWKV_COLS
D_FF = 11008
NFF = D_FF // 128
RMS_EPS = 1e-6
SUBLN_EPS = 1e-5
GN_EPS = 64e-5
NT = T // 128


class Src:
    def __init__(self, name, sem):
        self.name = name
        self.sem = sem
        self.count = 0


class Eng(Src):
    def __init__(self, name, sem, eng):
        super().__init__(name, sem)
        self.eng = eng
        self.waited = {}

    def wait_for(self, src, val, raw=False):
        if val <= 0:
            return
        if src is self:
            if not raw or self.name == "pe" or val > self.count:
                return
        if self.waited.get(src, 0) >= val:
            return
        self.eng.wait_ge(src.sem, val)
        self.waited[src] = val


class Buf:
    __slots__ = ("w", "r", "name", "shared", "ws")

    def __init__(self, name="", shared=False):
        self.w = None
        self.r = {}
        self.ws = {}
        self.name = name
        self.shared = shared


class K:
    def __init__(self, nc, es):
        self.nc = nc
        self.es = es

        def sem(n):
            return es.enter_context(nc.semaphore(n))

        self.PE = Eng("pe", sem("s_pe"), nc.tensor)
        self.ACT = Eng("act", sem("s_act"), nc.scalar)
        self.DVE = Eng("dve", sem("s_dve"), nc.vector)
        self.POOL = Eng("pool", sem("s_pool"), nc.gpsimd)
        self.SP = Eng("sp", sem("s_sp"), nc.sync)
        self.lanes = {
            self.SP: [Src(f"lsp{i}", sem(f"s_lsp{i}")) for i in range(20)],
            self.POOL: [Src(f"lpl{i}", sem(f"s_lpl{i}")) for i in range(8)],
            self.ACT: [Src(f"lac{i}", sem(f"s_lac{i}")) for i in range(8)],
        }
        self.lane_rr = {self.SP: 0, self.POOL: 0, self.ACT: 0}
        self.n_ins = 0
        self.uid = 0

    def _deps(self, E, r, w):
        for b in r:
            if b.shared:
                for s_, v in b.ws.items():
                    E.wait_for(s_, v)
            elif b.w is not None:
                E.wait_for(*b.w, raw=True)
        for b in w:
            if not b.shared and b.w is not None:
                E.wait_for(*b.w)
            for s_, v in b.r.items():
                E.wait_for(s_, v)

    def _record(self, src, val, r, w):
        for b in r:
            if b.r.get(src, 0) < val:
                b.r[src] = val
        for b in w:
            if b.shared:
                if b.ws.get(src, 0) < val:
                    b.ws[src] = val
            else:
                b.w = (src, val)
                b.r = {}

    def op(self, E, fn, r=(), w=(), inc=True):
        self._deps(E, r, w)
        ins = fn(E.eng)
        self.n_ins += 1
        if inc:
            ins.then_inc(E.sem, 1)
            E.count += 1
            val = E.count
        else:
            val = E.count + 1
        self._record(E, val, r, w)
        return ins

    def dma(self, out, in_, r=(), w=(), q=None, **kw):
        Q = q or self.SP
        self._deps(Q, r, w)
        lanes = self.lanes[Q]
        lane = lanes[self.lane_rr[Q] % len(lanes)]
        self.lane_rr[Q] += 1
        Q.wait_for(lane, lane.count)
        Q.eng.dma_start(out=out, in_=in_, **kw).then_inc(lane.sem, 16)
        self.n_ins += 1
        lane.count += 16
        self._record(lane, lane.count, r, w)

    def barrier(self):
        srcs = [self.PE, self.ACT, self.DVE, self.POOL]
        for lanes in self.lanes.values():
            srcs.extend(lanes)
        for E in (self.PE, self.ACT, self.DVE, self.POOL, self.SP):
            for s_ in srcs:
                E.wait_for(s_, s_.count)

    def finish(self):
        for Q, lanes in self.lanes.items():
            for lane in lanes:
                self.SP.wait_for(lane, lane.count)
        for E in (self.PE, self.ACT, self.DVE, self.POOL):
            self.SP.wait_for(E, E.count)

    def sb(self, es, name, shape, dt):
        self.uid += 1
        return es.enter_context(self.nc.sbuf_tensor(f"{name}_u{self.uid}", list(shape), dt))

    def ps(self, es, name, shape, dt=F32):
        self.uid += 1
        return es.enter_context(self.nc.psum_tensor(f"{name}_u{self.uid}", list(shape), dt))


class Ctx:
    pass


def build_program(dbg=None, nlayers=NL, phases=None, only_inputs=None, scan_steps=None, only_scratch=None, scan_cut=99):
    dbg = dbg or set()
    nc = bass.Bass("TRN2", target_bir_lowering=False)
    g = Ctx()
    g.nc = nc
    g.bufs = {}
    g.scan_steps = scan_steps
    g.scan_cut = scan_cut

    def din(name, shape, dt=F32):
        if only_inputs is not None and name not in only_inputs:
            return None
        return nc.dram_tensor(name, list(shape), dt, kind="ExternalInput").ap()

    def dscr(name, shape, dt=F32):
        if only_scratch is not None and name not in only_scratch:
            return None
        kind = "ExternalOutput" if name in dbg else "Internal"
        return nc.dram_tensor(name, list(shape), dt, kind=kind).ap()

    g.x = din("x", [T, D])
    g.c = din("c", [D])
    g.tmask = din("tmask", [T])
    g.ident = din("ident", [128, 128])
    g.ada_w = din("ada_w", [NL, D, 6 * D])
    g.ada_b = din("ada_b", [NL, 6 * D])
    g.norm1_g = din("norm1_g", [NL, D])
    g.w_in = din("w_in", [NL, D, IN_COLS])
    g.bkconst = din("bkconst", [128, 1152])
    g.att_lambda = din("att_lambda", [NL, 4, 64])
    g.att_subln_g = din("att_subln_g", [NL, 128])
    g.rel_bias = din("rel_bias", [32, 16])
    g.w_out = din("w_out", [NL, D, D])
    g.norm2_g = din("norm2_g", [NL, D])
    g.ffn_up = din("ffn_up", [NL, D, 2 * D_FF])
    g.ffn_conv = din("ffn_conv", [NL, 3, 2 * D_FF])
    g.ffn_down = din("ffn_down", [NL, D_FF, D])
    g.final_g = din("final_g", [D])
    g.rwkv_mu = din("rwkv_mu", [NL, 2, RWKV_COLS])
    g.rwkv_w0 = din("rwkv_w0", [NL, 2, C_RWKV])
    g.rwkv_w_up = din("rwkv_w_up", [NL, 2, R_W, C_RWKV])
    g.rwkv_a0 = din("rwkv_a0", [NL, 2, C_RWKV])
    g.rwkv_a_up = din("rwkv_a_up", [NL, 2, R_A, C_RWKV])
    g.rwkv_g_up = din("rwkv_g_up", [NL, R_G, C_RWKV])
    g.rwkv_k_k = din("rwkv_k_k", [NL, C_RWKV])
    g.rwkv_k_a = din("rwkv_k_a", [NL, C_RWKV])
    g.rwkv_r_k = din("rwkv_r_k", [NL, NH_RWKV, 64])
    g.rwkv_ln_g = din("rwkv_ln_g", [NL, C_RWKV])
    g.rwkv_ln_b = din("rwkv_ln_b", [NL, C_RWKV])
    g.onesbd = din("onesbd", [128, 128])
    g.trimask = din("trimask", [128, 4, 128])
    g.y = nc.dram_tensor("y", [T, D], F32, kind="ExternalOutput").ap()

    g.modb = dscr("modb", [6, 128, D])
    g.hT = dscr("hT", [KC, 128, T + 2], BF16)
    g.qkT = dscr("qkT", [32, 128, T], BF16)
    g.vtok = dscr("vtok", [T, C_ATT], BF16)
    g.zrT = dscr("zrT", [6656, T + 2], F32)
    g.Gd = dscr("Gd", [NH_ATT, 128, 1152])
    g.mixT = dscr("mixT", [KC, 128, T], BF16)
    g.aT = dscr("aT", [NFF, 128, T], BF16)
    g.rk_ops = dscr("rk_ops", [2, 6, 16, 128, T], BF16)
    g.rk_gc = dscr("rk_gc", [2, 16, 128, NT])
    g.rk_vT = dscr("rk_vT", [16, 128, T], BF16)
    g.rk_g = dscr("rk_g", [16, 128, T])
    g.rk_bv = dscr("rk_bv", [16, 128, T])
    g.rk_y = dscr("rk_y", [2, T, C_RWKV])
    g.xa = dscr("xa", [T, D])
    g.xb = dscr("xb", [T, D])

    with ExitStack() as es:
        k = K(nc, es)
        g.k = k
        ph = phases or {"ada", "norm1", "win", "attn", "rwkv", "wout", "norm2", "ffnup", "ffndown", "final"}
        bx = g.bufs.setdefault('x', Buf('x', True))
        bxa = g.bufs.setdefault('xa', Buf('xa', True))
        bxb = g.bufs.setdefault('xb', Buf('xb', True))
        if "attn" in ph:
            phase_attn_bias(g)
        xcur, bcur = g.x, bx
        for l in range(nlayers):
            if "ada" in ph:
                phase_ada(g, l)
            if "norm1" in ph:
                phase_norm(g, l, xcur, bcur, g.norm1_g, 0, zero_halo=(l == 0))
            if "win" in ph:
                phase_win(g, l)
            if "attn" in ph:
                phase_attn(g, l)
            if "rwkv" in ph or "rwkv_prep" in ph:
                phase_rwkv_prep(g, l)
            if "rwkv" in ph or "rwkv_scan" in ph:
                phase_rwkv_scan(g, l)
            if "rwkv" in ph or "rwkv_post" in ph:
                phase_rwkv_post(g, l)
            if "wout" in ph:
                phase_reslinear(g, l, g.mixT, g.bufs.setdefault("mixT", Buf("mixT", True)), KC, 0, g.w_out[l], 2,
                                xcur, bcur, g.xa, bxa)
            if "norm2" in ph:
                phase_norm(g, l, g.xa, bxa, g.norm2_g, 1)
            if "ffnup" in ph:
                phase_ffnup(g, l)
            if "ffndown" in ph:
                phase_reslinear(g, l, g.aT, g.bufs.setdefault("aT", Buf("aT", True)), NFF, 0, g.ffn_down[l], 5,
                                g.xa, bxa, g.xb, bxb)
            xcur, bcur = g.xb, bxb
        if "final" in ph:
            phase_final(g, xcur, bcur)
        k.finish()
        g.n_ins = k.n_ins
    nc._n_ins = k.n_ins
    return nc


def phase_ada(g, l):
    nc, k = g.nc, g.k
    k.barrier()
    with ExitStack() as es:
        craw = k.sb(es, "ada_c", [128, KC], F32)
        sc = k.sb(es, "ada_sc", [128, KC], F32)
        scb = k.sb(es, "ada_scb", [128, KC, 128], F32)
        wt = [k.sb(es, f"ada_w{i}", [128, 8, 512], F32) for i in range(2)]
        bb = [k.sb(es, f"ada_b{i}", [128, 512], F32) for i in range(2)]
        ot = [k.sb(es, f"ada_o{i}", [128, 512], F32) for i in range(2)]
        pst = [k.ps(es, f"ada_ps{i}", [128, 512]) for i in range(2)]
        b_c, b_sc, b_scb = Buf(), Buf(), Buf()
        b_wt = [Buf(), Buf()]
        b_bb = [Buf(), Buf()]
        b_ot = [Buf(), Buf()]
        b_ps = [Buf(), Buf()]
        b_modb = g.bufs.setdefault("modb", Buf("modb", True))
        k.dma(craw[:], g.c.rearrange("(kc p) -> p kc", p=128), w=[b_c], allow_slow_non_contiguous=True)
        k.op(k.ACT, lambda e: e.activation(sc[:], craw[:], AF.Silu), r=[b_c], w=[b_sc])
        k.op(k.DVE, lambda e: e.tensor_copy(scb[:], sc[:].unsqueeze(2).to_broadcast([128, KC, 128])),
             r=[b_sc], w=[b_scb])
        wv = g.ada_w[l].rearrange("(kc p) n -> p kc n", p=128)
        NB = 6 * D // 512
        for nb in range(NB):
            i = nb % 2
            cs = slice(nb * 512, (nb + 1) * 512)
            k.dma(bb[i][:], g.ada_b[l:l + 1, cs].partition_broadcast(128), w=[b_bb[i]])
            for q4 in range(4):
                wi = (nb * 4 + q4) % 2
                k.dma(wt[wi][:], wv[:, q4 * 8:(q4 + 1) * 8, cs], w=[b_wt[wi]])
                for kk in range(8):
                    kc = q4 * 8 + kk
                    last = (kc == KC - 1)
                    k.op(k.PE, lambda e: e.matmul(pst[i][:], scb[:, kc, :], wt[wi][:, kk, :],
                                                  start=(kc == 0), stop=last),
                         r=[b_scb, b_wt[wi]], w=[b_ps[i]], inc=(last or kk == 7))
            k.op(k.DVE, lambda e: e.tensor_tensor(ot[i][:], pst[i][:], bb[i][:], ALU.add),
                 r=[b_ps[i], b_bb[i]], w=[b_ot[i]])
            j, off = divmod(nb * 512, D)
            k.dma(g.modb[j, :, off:off + 512], ot[i][:], r=[b_ot[i]], w=[b_modb])


def phase_norm(g, l, xsrc, b_x, gvec, which, zero_halo=False):
    nc, k = g.nc, g.k
    b_modb = g.bufs.setdefault("modb", Buf("modb", True))
    b_hT = g.bufs.setdefault("hT", Buf("hT", True))
    k.barrier()
    with ExitStack() as es:
        G = k.sb(es, "nm_G", [128, D], F32)
        SH = k.sb(es, "nm_SH", [128, D], F32)
        gb = k.sb(es, "nm_gb", [128, D], F32)
        idf = k.sb(es, "nm_idf", [128, 128], F32)
        idb = k.sb(es, "nm_idb", [128, 128], BF16)
        xt = [k.sb(es, f"nm_x{i}", [128, D], F32) for i in range(2)]
        junk = k.sb(es, "nm_junk", [128, D], BF16)
        tmp = k.sb(es, "nm_tmp", [128, D], F32)
        hb = [k.sb(es, f"nm_hb{i}", [128, D], BF16) for i in range(2)]
        hTt = [k.sb(es, f"nm_hT{i}", [128, KC, 128], BF16) for i in range(2)]
        ss = [k.sb(es, f"nm_ss{i}", [128, 1], F32) for i in range(2)]
        rs = [k.sb(es, f"nm_rs{i}", [128, 1], F32) for i in range(2)]
        pst = [k.ps(es, f"nm_ps{i}", [128, 1024], BF16) for i in range(4)]
        b_G, b_SH, b_gb, b_idf, b_idb, b_junk, b_tmp = (Buf() for _ in range(7))
        b_xt = [Buf(), Buf()]
        b_hb = [Buf(), Buf()]
        b_hTt = [Buf(), Buf()]
        b_ss = [Buf(), Buf()]
        b_rs = [Buf(), Buf()]
        b_ps = [Buf() for _ in range(4)]
        k.dma(gb[:], gvec[l:l + 1, :].partition_broadcast(128), w=[b_gb])
        k.dma(G[:], g.modb[3 * which + 1], r=[b_modb], w=[b_G])
        k.dma(SH[:], g.modb[3 * which], r=[b_modb], w=[b_SH])
        k.dma(idf[:], g.ident, w=[b_idf])
        tmk = k.sb(es, "nm_tmk", [128, NT], F32)
        b_tmk = Buf()
        k.dma(tmk[:], g.tmask.rearrange("(i p) -> p i", p=128), w=[b_tmk], allow_slow_non_contiguous=True)
        k.op(k.DVE, lambda e: e.tensor_copy(idb[:], idf[:]), r=[b_idf], w=[b_idb])
        k.op(k.DVE, lambda e: e.scalar_tensor_tensor(G[:], G[:], 1.0, gb[:], ALU.add, ALU.mult),
             r=[b_G, b_gb], w=[b_G])
        if zero_halo:
            k.op(k.DVE, lambda e: e.memset(junk[:, 0:KC], 0.0), w=[b_junk])
            for col in (0, T + 1):
                k.dma(g.hT[:, :, col:col + 1].rearrange("kc p t -> p kc t"), junk[:, 0:KC].unsqueeze(2),
                      r=[b_junk], w=[b_hT], allow_slow_non_contiguous=True)
        for i in range(NT):
            j = i % 2
            k.dma(xt[j][:], xsrc[i * 128:(i + 1) * 128, :], r=[b_x], w=[b_xt[j]])
            k.op(k.ACT, lambda e: e.activation(junk[:], xt[j][:], AF.Square, accum_out=ss[j][:]),
                 r=[b_xt[j]], w=[b_junk, b_ss[j]])
            k.op(k.DVE, lambda e: e.tensor_scalar(rs[j][:], ss[j][:], 1.0 / D, RMS_EPS, ALU.mult, ALU.add),
                 r=[b_ss[j]], w=[b_rs[j]])
            k.op(k.ACT, lambda e: e.sqrt(rs[j][:], rs[j][:]), r=[b_rs[j]], w=[b_rs[j]])
            k.op(k.DVE, lambda e: e.reciprocal(rs[j][:], rs[j][:]), r=[b_rs[j]], w=[b_rs[j]])
            k.op(k.DVE, lambda e: e.tensor_tensor(rs[j][:], rs[j][:], tmk[:, i:i + 1], ALU.mult),
                 r=[b_rs[j], b_tmk], w=[b_rs[j]])
            k.op(k.DVE, lambda e: e.scalar_tensor_tensor(tmp[:], xt[j][:], rs[j][:], G[:], ALU.mult, ALU.mult),
                 r=[b_xt[j], b_rs[j], b_G], w=[b_tmp])
            k.op(k.DVE, lambda e: e.scalar_tensor_tensor(hb[j][:], SH[:], tmk[:, i:i + 1], tmp[:], ALU.mult, ALU.add),
                 r=[b_tmp, b_SH, b_tmk], w=[b_hb[j]])
            for q in range(4):
                pi = (i * 4 + q) % 4
                for kk in range(8):
                    kc = q * 8 + kk
                    k.op(k.PE, lambda e: e.transpose(pst[pi][:, kk * 128:(kk + 1) * 128],
                                                     hb[j][:, kc * 128:(kc + 1) * 128], idb[:]),
                         r=[b_hb[j], b_idb], w=[b_ps[pi]], inc=(kk == 7))
                E = k.ACT if q % 2 == 0 else k.DVE
                if E is k.ACT:
                    k.op(E, lambda e: e.copy(hTt[j][:, q * 8:(q + 1) * 8, :], pst[pi][:].rearrange("p (a b) -> p a b", a=8)),
                         r=[b_ps[pi]], w=[b_hTt[j]])
                else:
                    k.op(E, lambda e: e.tensor_copy(hTt[j][:, q * 8:(q + 1) * 8, :], pst[pi][:].rearrange("p (a b) -> p a b", a=8)),
                         r=[b_ps[pi]], w=[b_hTt[j]])
            k.dma(g.hT[:, :, 1 + i * 128:1 + (i + 1) * 128].rearrange("kc p t -> p kc t"), hTt[j][:],
                  r=[b_hTt[j]], w=[b_hT])


def phase_win(g, l):
    nc, k = g.nc, g.k
    b_hT = g.bufs.setdefault("hT", Buf("hT", True))
    b_qkT = g.bufs.setdefault("qkT", Buf("qkT", True))
    b_vtok = g.bufs.setdefault("vtok", Buf("vtok", True))
    b_zrT = g.bufs.setdefault("zrT", Buf("zrT", True))
    wv = g.w_in[l].rearrange("(kc p) n -> p kc n", p=128)
    segs = []
    for i in range(8):
        segs.append(("qk", i * 512, [128] * 4))
    for i in range(4):
        segs.append(("v", 4096 + i * 512, [512]))
    for i in range(12):
        segs.append(("r", 6144 + i * 512, [128] * 4))
    segs.append(("r", 12288, [96, 96, 128, 128]))
    TB = 1024
    k.barrier()
    with ExitStack() as es:
        hTs = k.sb(es, "wi_hT", [128, KC, TB], BF16)
        wb = [k.sb(es, f"wi_w{i}", [128, KC, 512], BF16) for i in range(2)]
        of = [k.sb(es, f"wi_of{i}", [128, 512], F32) for i in range(3)]
        ob = [k.sb(es, f"wi_ob{i}", [128, 512], BF16) for i in range(3)]
        pst = [k.ps(es, f"wi_ps{i}", [128, 512]) for i in range(6)]
        b_hTs = Buf()
        b_wb = [Buf(), Buf()]
        b_of = [Buf() for _ in range(3)]
        b_ob = [Buf() for _ in range(3)]
        b_ps = [Buf() for _ in range(6)]
        zt = k.sb(es, "wi_zero", [128, 52], F32)
        b_zt = Buf()
        if l == 0:
            k.op(k.DVE, lambda e: e.memset(zt[:], 0.0), w=[b_zt])
            for col in (0, T + 1):
                k.dma(g.zrT[0:6656, col:col + 1].rearrange("(a p) t -> p a t", p=128), zt[:].unsqueeze(2),
                      r=[b_zt], w=[b_zrT], allow_slow_non_contiguous=True)
        wcount = 0
        pcount = 0
        ocount = 0
        for tb in range(T // TB):
            k.dma(hTs[:], g.hT[:, :, 1 + tb * TB:1 + (tb + 1) * TB].rearrange("kc p t -> p kc t"),
                  r=[b_hT], w=[b_hTs])
            for (kind, c0, subs) in segs:
                wi = wcount % 2
                wcount += 1
                wd = sum(subs)
                k.dma(wb[wi][:, :, 0:wd], wv[:, :, c0:c0 + wd], w=[b_wb[wi]], q=k.POOL)
                if kind == "v":
                    for tt in range(TB // 128):
                        pi = pcount % 6
                        pcount += 1
                        for kc in range(KC):
                            k.op(k.PE, lambda e: e.matmul(pst[pi][:], hTs[:, kc, tt * 128:(tt + 1) * 128],
                                                          wb[wi][:, kc, :], start=(kc == 0), stop=(kc == KC - 1)),
                                 r=[b_hTs, b_wb[wi]], w=[b_ps[pi]], inc=(kc == KC - 1))
                        oi = ocount % 3
                        ocount += 1
                        if oi % 2 == 0:
                            k.op(k.ACT, lambda e: e.copy(ob[oi][:], pst[pi][:]), r=[b_ps[pi]], w=[b_ob[oi]])
                        else:
                            k.op(k.DVE, lambda e: e.tensor_copy(ob[oi][:], pst[pi][:]), r=[b_ps[pi]], w=[b_ob[oi]])
                        t0 = tb * TB + tt * 128
                        k.dma(g.vtok[t0:t0 + 128, c0 - 4096:c0 - 4096 + 512], ob[oi][:], r=[b_ob[oi]], w=[b_vtok])
                    continue
                off = 0
                for m in subs:
                    p0 = pcount % 6
                    p1 = (pcount + 1) % 6
                    pcount += 2
                    pp = (p0, p1)
                    for kc in range(KC):
                        for th in range(2):
                            k.op(k.PE, lambda e: e.matmul(pst[pp[th]][0:m, :], wb[wi][:, kc, off:off + m],
                                                          hTs[:, kc, th * 512:(th + 1) * 512],
                                                          start=(kc == 0), stop=(kc == KC - 1)),
                                 r=[b_hTs, b_wb[wi]], w=[b_ps[pp[th]]], inc=(kc == KC - 1))
                    for th in range(2):
                        oi = ocount % 3
                        ocount += 1
                        t0 = tb * TB + th * 512
                        if kind == "qk":
                            dst, bdst = ob[oi], b_ob[oi]
                            dram = g.qkT[(c0 + off) // 128, :, t0:t0 + 512]
                            bd = b_qkT
                        else:
                            dst, bdst = of[oi], b_of[oi]
                            r0 = c0 + off - 6144
                            dram = g.zrT[r0:r0 + m, 1 + t0:1 + t0 + 512]
                            bd = b_zrT
                        if oi % 2 == 0:
                            k.op(k.ACT, lambda e: e.copy(dst[0:m, :], pst[pp[th]][0:m, :]), r=[b_ps[pp[th]]], w=[bdst])
                        else:
                            k.op(k.DVE, lambda e: e.tensor_copy(dst[0:m, :], pst[pp[th]][0:m, :]), r=[b_ps[pp[th]]], w=[bdst])
                        k.dma(dram, dst[0:m, :], r=[bdst], w=[bd])
                    off += m


def phase_reslinear(g, l, srcT, b_src, nk, src_off, W, gate_j, xin, b_xin, xout, b_xout):
    nc, k = g.nc, g.k
    b_modb = g.bufs.setdefault("modb", Buf("modb", True))
    wv = W.rearrange("(kc p) n -> p kc n", p=128)
    k.barrier()
    with ExitStack() as es:
        GATE = k.sb(es, "rl_gate", [128, D], F32)
        nwb = 2 if nk <= 32 else 1
        wb = [k.sb(es, f"rl_w{i}", [128, nk, 512], BF16) for i in range(nwb)]
        st = [k.sb(es, f"rl_s{i}", [128, nk, 128], BF16) for i in range(2)]
        xi = [k.sb(es, f"rl_xi{i}", [128, 512], F32) for i in range(2)]
        xo = [k.sb(es, f"rl_xo{i}", [128, 512], F32) for i in range(2)]
        pst = [k.ps(es, f"rl_ps{i}", [128, 512]) for i in range(2)]
        b_gate = Buf()
        b_wb = [Buf() for _ in range(nwb)]
        b_st = [Buf(), Buf()]
        b_xi = [Buf(), Buf()]
        b_xo = [Buf(), Buf()]
        b_ps = [Buf(), Buf()]
        k.dma(GATE[:], g.modb[gate_j], r=[b_modb], w=[b_gate])
        cnt = 0
        for cb in range(D // 512):
            wi = cb % nwb
            k.dma(wb[wi][:], wv[:, :, cb * 512:(cb + 1) * 512], w=[b_wb[wi]], q=k.POOL)
            for tt in range(NT):
                j = cnt % 2
                cnt += 1
                t0 = tt * 128
                k.dma(st[j][:], srcT[:, :, src_off + t0:src_off + t0 + 128].rearrange("kc p t -> p kc t"),
                      r=[b_src], w=[b_st[j]])
                k.dma(xi[j][:], xin[t0:t0 + 128, cb * 512:(cb + 1) * 512], r=[b_xin], w=[b_xi[j]])
                for kc in range(nk):
                    k.op(k.PE, lambda e: e.matmul(pst[j][:], st[j][:, kc, :], wb[wi][:, kc, :],
                                                  start=(kc == 0), stop=(kc == nk - 1)),
                         r=[b_st[j], b_wb[wi]], w=[b_ps[j]], inc=(kc == nk - 1))
                k.op(k.DVE, lambda e: e.tensor_tensor(xo[j][:], pst[j][:], GATE[:, cb * 512:(cb + 1) * 512], ALU.mult),
                     r=[b_ps[j], b_gate], w=[b_xo[j]])
                k.op(k.DVE, lambda e: e.tensor_tensor(xo[j][:], xo[j][:], xi[j][:], ALU.add),
                     r=[b_xo[j], b_xi[j]], w=[b_xo[j]])
                k.dma(xout[t0:t0 + 128, cb * 512:(cb + 1) * 512], xo[j][:], r=[b_xo[j]], w=[b_xout])


def phase_ffnup(g, l):
    nc, k = g.nc, g.k
    b_hT = g.bufs.setdefault("hT", Buf("hT", True))
    b_aT = g.bufs.setdefault("aT", Buf("aT", True))
    wv = g.ffn_up[l].rearrange("(kc p) n -> p kc n", p=128)
    k.barrier()
    TB = 1024
    NS = TB // 512
    FG = 2
    with ExitStack() as es:
        hTs = k.sb(es, "fu_hT", [128, KC, NS, 514], BF16)
        wg = [k.sb(es, f"fu_wg{i}", [128, KC, FG * 128], BF16) for i in range(2)]
        wvv = [k.sb(es, f"fu_wv{i}", [128, KC, FG * 128], BF16) for i in range(2)]
        cw = k.sb(es, "fu_cw", [128, 3, 2 * NFF], F32)
        accg = [k.sb(es, f"fu_ag{i}", [128, 512], F32) for i in range(2)]
        accv = [k.sb(es, f"fu_av{i}", [128, 512], F32) for i in range(2)]
        sg = [k.sb(es, f"fu_sg{i}", [128, 512], F32) for i in range(2)]
        ao = [k.sb(es, f"fu_ao{i}", [128, 512], BF16) for i in range(2)]
        psg = [k.ps(es, f"fu_pg{i}", [128, 512]) for i in range(2)]
        psv = [k.ps(es, f"fu_pv{i}", [128, 512]) for i in range(2)]
        psh = [k.ps(es, f"fu_ph{i}", [128, 4]) for i in range(2)]
        b_hTs, b_cw = Buf(), Buf()
        b_wg = [Buf(), Buf()]
        b_wv = [Buf(), Buf()]
        b_ag = [Buf(), Buf()]
        b_av = [Buf(), Buf()]
        b_sg = [Buf(), Buf()]
        b_ao = [Buf(), Buf()]
        b_pg = [Buf(), Buf()]
        b_pv = [Buf(), Buf()]
        b_ph = [Buf(), Buf()]
        for tap in range(3):
            k.dma(cw[:, tap, :], g.ffn_conv[l, tap, :].rearrange("(j p) -> p j", p=128), w=[b_cw],
                  allow_slow_non_contiguous=True)
        wcnt = 0
        cnt = 0
        for tb in range(T // TB):
            for sidx in range(NS):
                t0 = tb * TB + sidx * 512
                k.dma(hTs[:, :, sidx, :], g.hT[:, :, t0:t0 + 514].rearrange("kc p t -> p kc t"),
                      r=[b_hT], w=[b_hTs])
            for fs in range(NFF // FG):
                wi = wcnt % 2
                wcnt += 1
                f0 = fs * FG * 128
                k.dma(wg[wi][:], wv[:, :, f0:f0 + FG * 128], w=[b_wg[wi]], q=k.POOL)
                k.dma(wvv[wi][:], wv[:, :, D_FF + f0:D_FF + f0 + FG * 128], w=[b_wv[wi]], q=k.POOL)
                for fi in range(FG):
                    f = fs * FG + fi
                    for sidx in range(NS):
                        j = cnt % 2
                        cnt += 1
                        t0 = tb * TB + sidx * 512
                        for kc in range(KC):
                            last = (kc == KC - 1)
                            lg = wg[wi][:, kc, fi * 128:(fi + 1) * 128]
                            lv = wvv[wi][:, kc, fi * 128:(fi + 1) * 128]
                            k.op(k.PE, lambda e: e.matmul(psg[j][:], lg, hTs[:, kc, sidx, 1:513], start=(kc == 0), stop=last),
                                 r=[b_hTs, b_wg[wi]], w=[b_pg[j]], inc=last)
                            k.op(k.PE, lambda e: e.matmul(psh[j][:, 0:2], lg, hTs[:, kc, sidx, 0:514:513], start=(kc == 0), stop=last),
                                 r=[b_hTs, b_wg[wi]], w=[b_ph[j]], inc=False)
                            k.op(k.PE, lambda e: e.matmul(psv[j][:], lv, hTs[:, kc, sidx, 1:513], start=(kc == 0), stop=last),
                                 r=[b_hTs, b_wv[wi]], w=[b_pv[j]], inc=last)
                            k.op(k.PE, lambda e: e.matmul(psh[j][:, 2:4], lv, hTs[:, kc, sidx, 0:514:513], start=(kc == 0), stop=last),
                                 r=[b_hTs, b_wv[wi]], w=[b_ph[j]], inc=last)
                        for (ps_, acc, b_p, b_a, col, hoff) in ((psg[j], accg[j], b_pg[j], b_ag[j], f, 0),
                                                                (psv[j], accv[j], b_pv[j], b_av[j], NFF + f, 2)):
                            w0 = cw[:, 0, col:col + 1]
                            w1 = cw[:, 1, col:col + 1]
                            w2 = cw[:, 2, col:col + 1]
                            k.op(k.ACT, lambda e: e.activation(acc[:], ps_[:], AF.Copy, scale=w1),
                                 r=[b_p, b_cw], w=[b_a])
                            k.op(k.DVE, lambda e: e.scalar_tensor_tensor(acc[:, 1:512], ps_[:, 0:511], w0, acc[:, 1:512], ALU.mult, ALU.add),
                                 r=[b_p, b_cw, b_a], w=[b_a])
                            k.op(k.DVE, lambda e: e.scalar_tensor_tensor(acc[:, 0:511], ps_[:, 1:512], w2, acc[:, 0:511], ALU.mult, ALU.add),
                                 r=[b_p, b_cw, b_a], w=[b_a])
                            k.op(k.DVE, lambda e: e.scalar_tensor_tensor(acc[:, 0:1], psh[j][:, hoff:hoff + 1], w0, acc[:, 0:1], ALU.mult, ALU.add),
                                 r=[b_ph[j], b_cw, b_a], w=[b_a])
                            k.op(k.DVE, lambda e: e.scalar_tensor_tensor(acc[:, 511:512], psh[j][:, hoff + 1:hoff + 2], w2, acc[:, 511:512], ALU.mult, ALU.add),
                                 r=[b_ph[j], b_cw, b_a], w=[b_a])
                        k.op(k.ACT, lambda e: e.activation(sg[j][:], accg[j][:], AF.Silu), r=[b_ag[j]], w=[b_sg[j]])
                        k.op(k.DVE, lambda e: e.tensor_tensor(ao[j][:], sg[j][:], accv[j][:], ALU.mult),
                             r=[b_sg[j], b_av[j]], w=[b_ao[j]])
                        k.dma(g.aT[f, :, t0:t0 + 512], ao[j][:], r=[b_ao[j]], w=[b_aT])


def phase_final(g, xin, b_xin):
    nc, k = g.nc, g.k
    b_y = g.bufs.setdefault("y", Buf("y", True))
    k.barrier()
    with ExitStack() as es:
        gb = k.sb(es, "fn_gb", [128, D], F32)
        xt = [k.sb(es, f"fn_x{i}", [128, D], F32) for i in range(2)]
        yt = [k.sb(es, f"fn_y{i}", [128, D], F32) for i in range(2)]
        junk = k.sb(es, "fn_junk", [128, D], BF16)
        ss = [k.sb(es, f"fn_ss{i}", [128, 1], F32) for i in range(2)]
        rs = [k.sb(es, f"fn_rs{i}", [128, 1], F32) for i in range(2)]
        b_gb, b_junk = Buf(), Buf()
        b_xt = [Buf(), Buf()]
        b_yt = [Buf(), Buf()]
        b_ss = [Buf(), Buf()]
        b_rs = [Buf(), Buf()]
        k.dma(gb[:], g.final_g.rearrange("(o n) -> o n", o=1).partition_broadcast(128), w=[b_gb])
        for i in range(NT):
            j = i % 2
            k.dma(xt[j][:], xin[i * 128:(i + 1) * 128, :], r=[b_xin], w=[b_xt[j]])
            k.op(k.ACT, lambda e: e.activation(junk[:], xt[j][:], AF.Square, accum_out=ss[j][:]),
                 r=[b_xt[j]], w=[b_junk, b_ss[j]])
            k.op(k.DVE, lambda e: e.tensor_scalar(rs[j][:], ss[j][:], 1.0 / D, RMS_EPS, ALU.mult, ALU.add),
                 r=[b_ss[j]], w=[b_rs[j]])
            k.op(k.ACT, lambda e: e.sqrt(rs[j][:], rs[j][:]), r=[b_rs[j]], w=[b_rs[j]])
            k.op(k.DVE, lambda e: e.reciprocal(rs[j][:], rs[j][:]), r=[b_rs[j]], w=[b_rs[j]])
            k.op(k.DVE, lambda e: e.scalar_tensor_tensor(yt[j][:], xt[j][:], rs[j][:], gb[:], ALU.mult, ALU.mult),
                 r=[b_xt[j], b_rs[j], b_gb], w=[b_yt[j]])
            k.dma(g.y[i * 128:(i + 1) * 128, :], yt[j][:], r=[b_yt[j]], w=[b_y])


def phase_attn_bias(g):
    nc, k = g.nc, g.k
    b_Gd = g.bufs.setdefault("Gd", Buf("Gd", True))
    k.barrier()
    with ExitStack() as es:
        BK = k.sb(es, "ab_bk", [128, 1152], F32)
        btab = k.sb(es, "ab_bt", [128, 512], F32)
        G = [k.sb(es, f"ab_g{i}", [128, 1152], F32) for i in range(2)]
        tmp = k.sb(es, "ab_tmp", [128, 1152], F32)
        b_BK, b_bt, b_tmp = Buf(), Buf(), Buf()
        b_G = [Buf(), Buf()]
        k.dma(BK[:], g.bkconst, w=[b_BK])
        k.dma(btab[:], g.rel_bias.rearrange("(o b) h -> o (b h)", o=1).partition_broadcast(128), w=[b_bt])
        for h in range(NH_ATT):
            j = h % 2
            for b in range(32):
                sc = btab[:, b * 16 + h:b * 16 + h + 1]
                if b == 0:
                    k.op(k.DVE, lambda e: e.tensor_scalar(G[j][:], BK[:], float(b), sc, ALU.is_equal, ALU.mult),
                         r=[b_BK, b_bt], w=[b_G[j]])
                else:
                    k.op(k.DVE, lambda e: e.tensor_scalar(tmp[:], BK[:], float(b), sc, ALU.is_equal, ALU.mult),
                         r=[b_BK, b_bt], w=[b_tmp])
                    k.op(k.DVE, lambda e: e.tensor_tensor(G[j][:], G[j][:], tmp[:], ALU.add),
                         r=[b_tmp, b_G[j]], w=[b_G[j]])
            k.dma(g.Gd[h], G[j][:], r=[b_G[j]], w=[b_Gd])


def phase_attn(g, l):
    nc, k = g.nc, g.k
    b_qkT = g.bufs.setdefault("qkT", Buf("qkT", True))
    b_vtok = g.bufs.setdefault("vtok", Buf("vtok", True))
    b_Gd = g.bufs.setdefault("Gd", Buf("Gd", True))
    b_mixT = g.bufs.setdefault("mixT", Buf("mixT", True))
    lam_init = 0.8 - 0.6 * math.exp(-0.3 * l)
    scale = 64 ** -0.5
    k.barrier()
    with ExitStack() as es:
        kts = [k.sb(es, f"at_k{i}", [128, T], BF16) for i in range(2)]
        qts = [k.sb(es, f"at_q{i}", [128, T], BF16) for i in range(2)]
        vts = [k.sb(es, f"at_v{i}", [128, NT, 128], BF16) for i in range(2)]
        Gs = [k.sb(es, f"at_g{i}", [128, 1152], F32) for i in range(2)]
        fb = [k.sb(es, f"at_fb{i}", [128, 2, NT], F32) for i in range(2)]
        b_kt = [Buf(), Buf()]
        b_qt = [Buf(), Buf()]
        b_vt = [Buf(), Buf()]
        b_Gs = [Buf(), Buf()]
        b_fb = [Buf(), Buf()]
        btab = k.sb(es, "at_bt", [128, 512], F32)
        kmb = k.sb(es, "at_kmb", [128, NT], F32)
        lp = k.sb(es, "at_lp", [128, 4, 64], F32)
        lt = k.sb(es, "at_lt", [128, 2, 64], F32)
        ld = k.sb(es, "at_ld", [128, 2], F32)
        nlam = k.sb(es, "at_nlam", [128, 1], F32)
        gsc = k.sb(es, "at_gsc", [128, 1], F32)
        ones = k.sb(es, "at_ones", [128, 128], BF16)
        b_bt, b_kmb, b_lp, b_lt, b_ld, b_nlam, b_gsc, b_ones = (Buf() for _ in range(8))
        P = [[k.sb(es, f"at_p{m}{i}", [128, 512], BF16) for i in range(2)] for m in range(2)]
        b_P = [[Buf(), Buf()], [Buf(), Buf()]]
        sbt = [k.sb(es, f"at_sb{i}", [128, 512], F32) for i in range(2)]
        b_sbt = [Buf(), Buf()]
        r0 = k.sb(es, "at_r0", [128, 512], F32)
        r1 = k.sb(es, "at_r1", [128, 512], F32)
        o0 = k.sb(es, "at_o0", [128, 512], F32)
        o1 = k.sb(es, "at_o1", [128, 512], F32)
        sq = k.sb(es, "at_sq", [128, 512], BF16)
        rn = k.sb(es, "at_rn", [128, 512], F32)
        ob = [k.sb(es, f"at_ob{i}", [128, 512], BF16) for i in range(2)]
        b_r0, b_r1, b_o0, b_o1, b_sq, b_rn = (Buf() for _ in range(6))
        b_ob = [Buf(), Buf()]
        psS = [[k.ps(es, f"at_ps{m}{i}", [128, 512]) for i in range(2)] for m in range(2)]
        b_psS = [[Buf(), Buf()], [Buf(), Buf()]]
        psO = [k.ps(es, f"at_po{m}", [128, 512]) for m in range(2)]
        psD = [k.ps(es, f"at_pd{m}", [128, 512]) for m in range(2)]
        b_psO = [Buf(), Buf()]
        b_psD = [Buf(), Buf()]

        k.dma(btab[:], g.rel_bias.rearrange("(o b) h -> o (b h)", o=1).partition_broadcast(128), w=[b_bt])
        k.dma(kmb[:], g.tmask.rearrange("(i p) -> p i", p=128), w=[b_kmb], allow_slow_non_contiguous=True)
        k.op(k.DVE, lambda e: e.tensor_scalar(kmb[:], kmb[:], -1.0, 30000.0, ALU.add, ALU.mult), r=[b_kmb], w=[b_kmb])
        k.dma(lp[:], g.att_lambda[l:l + 1].rearrange("o a d -> o (a d)").partition_broadcast(128), w=[b_lp])
        k.op(k.DVE, lambda e: e.tensor_tensor(lt[:, 0, :], lp[:, 0, :], lp[:, 1, :], ALU.mult), r=[b_lp], w=[b_lt])
        k.op(k.DVE, lambda e: e.tensor_tensor(lt[:, 1, :], lp[:, 2, :], lp[:, 3, :], ALU.mult), r=[b_lp, b_lt], w=[b_lt])
        k.op(k.DVE, lambda e: e.reduce_sum(ld[:], lt[:], AX.X), r=[b_lt], w=[b_ld])
        k.op(k.ACT, lambda e: e.activation(ld[:], ld[:], AF.Exp), r=[b_ld], w=[b_ld])
        k.op(k.DVE, lambda e: e.tensor_tensor(nlam[:], ld[:, 1:2], ld[:, 0:1], ALU.subtract), r=[b_ld], w=[b_nlam])
        k.op(k.DVE, lambda e: e.tensor_scalar_add(nlam[:], nlam[:], -lam_init), r=[b_nlam], w=[b_nlam])
        k.dma(gsc[:], g.att_subln_g[l].rearrange("(p o) -> p o", o=1), w=[b_gsc], allow_slow_non_contiguous=True)
        k.op(k.DVE, lambda e: e.tensor_scalar_mul(gsc[:], gsc[:], 1.0 - lam_init), r=[b_gsc], w=[b_gsc])
        k.op(k.DVE, lambda e: e.memset(ones[:], 1.0), w=[b_ones])

        scnt = 0
        for h in range(NH_ATT):
            j = h % 2
            k.dma(kts[j][:], g.qkT[16 + h], r=[b_qkT], w=[b_kt[j]])
            k.dma(qts[j][:], g.qkT[h], r=[b_qkT], w=[b_qt[j]])
            k.dma(vts[j][:], g.vtok[:, h * 128:(h + 1) * 128].rearrange("(i p) d -> p i d", p=128),
                  r=[b_vtok], w=[b_vt[j]])
            k.dma(Gs[j][:], g.Gd[h], r=[b_Gd], w=[b_Gs[j]])
            k.op(k.DVE, lambda e: e.tensor_scalar(fb[j][:, 0, :], kmb[:], btab[:, 15 * 16 + h:15 * 16 + h + 1], None, ALU.add),
                 r=[b_kmb, b_bt], w=[b_fb[j]])
            k.op(k.DVE, lambda e: e.tensor_scalar(fb[j][:, 1, :], kmb[:], btab[:, 31 * 16 + h:31 * 16 + h + 1], None, ALU.add),
                 r=[b_kmb, b_bt, b_fb[j]], w=[b_fb[j]])
            for qb in range(T // 512):
                for kt in range(NT):
                    delta = kt - 4 * qb
                    near = (-1 <= delta <= 4)
                    si = scnt % 2
                    scnt += 1
                    for m in range(2):
                        k.op(k.PE, lambda e: e.matmul(psS[m][si][:], kts[j][m * 64:(m + 1) * 64, kt * 128:(kt + 1) * 128],
                                                      qts[j][m * 64:(m + 1) * 64, qb * 512:(qb + 1) * 512],
                                                      start=True, stop=True),
                             r=[b_kt[j], b_qt[j]], w=[b_psS[m][si]])
                    for m in range(2):
                        if near:
                            c0 = 512 - delta * 128
                            k.op(k.DVE, lambda e: e.scalar_tensor_tensor(sbt[m][:], psS[m][si][:], scale, Gs[j][:, c0:c0 + 512],
                                                                         ALU.mult, ALU.add),
                                 r=[b_psS[m][si], b_Gs[j]], w=[b_sbt[m]])
                            k.op(k.ACT, lambda e: e.activation(P[m][si][:], sbt[m][:], AF.Exp, bias=kmb[:, kt:kt + 1]),
                                 r=[b_sbt[m], b_kmb], w=[b_P[m][si]])
                        else:
                            fi = 0 if delta < 0 else 1
                            k.op(k.ACT, lambda e: e.activation(P[m][si][:], psS[m][si][:], AF.Exp,
                                                               bias=fb[j][:, fi, kt:kt + 1], scale=scale),
                                 r=[b_psS[m][si], b_fb[j]], w=[b_P[m][si]])
                    for m in range(2):
                        k.op(k.PE, lambda e: e.matmul(psO[m][:], vts[j][:, kt, :], P[m][si][:],
                                                      start=(kt == 0), stop=(kt == NT - 1)),
                             r=[b_vt[j], b_P[m][si]], w=[b_psO[m]], inc=False)
                        k.op(k.PE, lambda e: e.matmul(psD[m][:], ones[:], P[m][si][:],
                                                      start=(kt == 0), stop=(kt == NT - 1)),
                             r=[b_ones, b_P[m][si]], w=[b_psD[m]], inc=True)
                k.op(k.DVE, lambda e: e.reciprocal(r0[:], psD[0][:]), r=[b_psD[0]], w=[b_r0])
                k.op(k.DVE, lambda e: e.reciprocal(r1[:], psD[1][:]), r=[b_psD[1]], w=[b_r1])
                k.op(k.DVE, lambda e: e.tensor_tensor(o0[:], psO[0][:], r0[:], ALU.mult), r=[b_psO[0], b_r0], w=[b_o0])
                k.op(k.DVE, lambda e: e.tensor_tensor(o1[:], psO[1][:], r1[:], ALU.mult), r=[b_psO[1], b_r1], w=[b_o1])
                k.op(k.DVE, lambda e: e.scalar_tensor_tensor(o0[:], o1[:], nlam[:, 0:1], o0[:], ALU.mult, ALU.add),
                     r=[b_o1, b_nlam, b_o0], w=[b_o0])
                k.op(k.ACT, lambda e: e.activation(sq[:], o0[:], AF.Square), r=[b_o0], w=[b_sq])
                pn = psS[0][scnt % 2]
                b_pn = b_psS[0][scnt % 2]
                k.op(k.PE, lambda e: e.matmul(pn[:], ones[:], sq[:], start=True, stop=True), r=[b_ones, b_sq], w=[b_pn])
                k.op(k.DVE, lambda e: e.tensor_scalar(rn[:], pn[:], 1.0 / 128, SUBLN_EPS, ALU.mult, ALU.add), r=[b_pn], w=[b_rn])
                k.op(k.ACT, lambda e: e.sqrt(rn[:], rn[:]), r=[b_rn], w=[b_rn])
                k.op(k.DVE, lambda e: e.reciprocal(rn[:], rn[:]), r=[b_rn], w=[b_rn])
                oj = qb % 2
                k.op(k.DVE, lambda e: e.scalar_tensor_tensor(ob[oj][:], o0[:], gsc[:, 0:1], rn[:], ALU.mult, ALU.mult),
                     r=[b_o0, b_gsc, b_rn], w=[b_ob[oj]])
                k.dma(g.mixT[h, :, qb * 512:(qb + 1) * 512], ob[oj][:], r=[b_ob[oj]], w=[b_mixT])


def _bucket_const():
    kl = np.arange(128)[:, None]
    c = np.arange(1152)[None, :]
    rel = kl - c + 512
    nb = 16
    max_exact = 8
    n = np.abs(rel)
    nf = np.maximum(n, 1).astype(np.float32)
    large = max_exact + (np.log(nf / max_exact) / np.float32(math.log(128 / max_exact)) * (nb - max_exact)).astype(np.int32)
    large = np.minimum(large, nb - 1)
    bk = np.where(rel > 0, nb, 0) + np.where(n < max_exact, n, large)
    return bk.astype(np.float32)


def make_inputs(d, x, c, tm):
    im = {"x": np.ascontiguousarray(x, dtype=np.float32), "c": np.ascontiguousarray(c, dtype=np.float32),
          "tmask": tm, "ident": np.eye(128, dtype=np.float32), "bkconst": _bucket_const()}
    ii = np.arange(128)
    su = (ii[:, None] < ii[None, :]).astype(np.float32)
    iu = (ii[:, None] <= ii[None, :]).astype(np.float32)
    im["trimask"] = np.ascontiguousarray(np.stack([su, iu, su.T, iu.T], axis=1))
    im["onesbd"] = np.kron(np.eye(2, dtype=np.float32), np.ones((64, 64), np.float32))
    for name in ("ada_w", "ada_b", "norm1_g", "w_in", "att_lambda", "att_subln_g", "rel_bias", "w_out", "norm2_g",
                 "ffn_up", "ffn_conv", "ffn_down", "final_g", "rwkv_mu", "rwkv_w0", "rwkv_w_up", "rwkv_a0", "rwkv_a_up",
                 "rwkv_g_up", "rwkv_k_k", "rwkv_k_a", "rwkv_r_k", "rwkv_ln_g", "rwkv_ln_b"):
        im[name] = np.ascontiguousarray(d[name], dtype=np.float32)
    return im


LWS = 0.6065306597126334


class TP:
    def __init__(self, k, es, name, shape, dt, n, psum=False):
        mk = k.ps if psum else k.sb
        self.t = [mk(es, f"{name}{i}", shape, dt) for i in range(n)]
        self.b = [Buf() for _ in range(n)]
        self.i = 0

    def get(self):
        j = self.i % len(self.t)
        self.i += 1
        return self.t[j], self.b[j]


def phase_rwkv_prep(g, l):
    nc, k = g.nc, g.k
    b_zrT = g.bufs.setdefault("zrT", Buf("zrT", True))
    b_ops = g.bufs.setdefault("rk_ops", Buf("rk_ops", True))
    b_gc = g.bufs.setdefault("rk_gc", Buf("rk_gc", True))
    b_vT = g.bufs.setdefault("rk_vT", Buf("rk_vT", True))
    b_g = g.bufs.setdefault("rk_g", Buf("rk_g", True))
    b_bv = g.bufs.setdefault("rk_bv", Buf("rk_bv", True))
    k.barrier()
    W = 512
    with ExitStack() as es:
        def tile(name, shape, dt=F32):
            return k.sb(es, "rp_" + name, shape, dt), Buf()

        mu3, b_mu3 = tile("mu3", [128, 2, 48])
        muc3, b_muc3 = tile("muc3", [128, 48])
        mul, b_mul = tile("mul", [128, 2, 4])
        mucl, b_mucl = tile("mucl", [128, 4])
        w0c, b_w0c = tile("w0c", [128, 2, 16])
        a0c, b_a0c = tile("a0c", [128, 2, 16])
        kkc, b_kkc = tile("kkc", [128, 16])
        kac, b_kac = tile("kac", [128, 16])
        omka, b_omka = tile("omka", [128, 16])
        rkc, b_rkc = tile("rkc", [128, 16])
        wup, b_wup = tile("wup", [128, 2, 2048], BF16)
        aup, b_aup = tile("aup", [128, 2, 2048], BF16)
        gup, b_gup = tile("gup", [128, 2, 2048], BF16)
        onesbd, b_onesbd = tile("onesbd", [128, 128])
        tri, b_tri = tile("tri", [128, 4, 128])
        idf, b_idf = tile("idf", [128, 128])
        zxw, b_zxw = tile("zxw", [128, W + 2])
        zxa, b_zxa = tile("zxa", [128, W + 2])
        zxg, b_zxg = tile("zxg", [128, 2, W + 2])
        lt, b_lt = tile("lt", [128, W])
        tw, b_tw = tile("tw", [128, W], BF16)
        xab, b_xab = tile("xab", [128, W], BF16)
        sgb, b_sgb = tile("sgb", [128, 2, W], BF16)
        zin = [tile(f"zin{i}", [128, W + 2]) for i in range(3)]
        zs = [tile(f"zs{i}", [128, W]) for i in range(3)]
        ld = [tile(f"ld{i}", [128, W]) for i in range(2)]
        aa = [tile(f"aa{i}", [128, W]) for i in range(2)]
        kd = [tile(f"kd{i}", [128, W]) for i in range(2)]
        gf, b_gf = tile("gf", [128, W])
        kk, b_kk = tile("kk", [128, W])
        sq, b_sq = tile("sq", [128, W])
        rn, b_rn = tile("rn", [128, W])
        tk, b_tk = tile("tk", [128, W])
        bt, b_bt = tile("bt", [128, W])
        bv, b_bv_t = tile("bv", [128, W])
        vb, b_vb = tile("vb", [128, W], BF16)
        ldT, b_ldT = tile("ldT", [128, W])
        cum, b_cum = tile("cum", [128, W])
        ecp, b_ecp = tile("ecp", [128, W])
        ecm, b_ecm = tile("ecm", [128, W])
        cex, b_cex = tile("cex", [128, W])
        ect, b_ect = tile("ect", [128, W])
        ba, b_ba = tile("ba", [128, W])
        tot, b_tot = tile("tot", [128, 4])
        gcv, b_gcv = tile("gcv", [128, 4])
        outs = TP(k, es, "rp_out", [128, W], BF16, 6)
        tmb, b_tmb = tile("tmb", [128, T])
        k.dma(tmb[:], g.tmask.rearrange("(o t) -> o t", o=1).partition_broadcast(128), w=[b_tmb])
        PS = TP(k, es, "rp_ps", [128, 512], F32, 8, psum=True)

        k.op(k.DVE, lambda e: e.memset(mul[:], 0.0), w=[b_mul])
        for d in range(2):
            k.dma(mu3[:, d, :], g.rwkv_mu[l, d, 0:6144].rearrange("(j p) -> p j", p=128), w=[b_mu3],
                  allow_slow_non_contiguous=True)
            k.dma(mul[0:96, d, 0:1], g.rwkv_mu[l, d, 6144:6240].rearrange("(p o) -> p o", o=1), w=[b_mul],
                  allow_slow_non_contiguous=True)
            k.dma(mul[0:96, d, 1:2], g.rwkv_mu[l, d, 6240:6336].rearrange("(p o) -> p o", o=1), w=[b_mul],
                  allow_slow_non_contiguous=True)
            k.dma(mul[:, d, 2:4], g.rwkv_mu[l, d, 6336:6592].rearrange("(j p) -> p j", p=128), w=[b_mul],
                  allow_slow_non_contiguous=True)
            k.dma(w0c[:, d, :], g.rwkv_w0[l, d].rearrange("(j p) -> p j", p=128), w=[b_w0c], allow_slow_non_contiguous=True)
            k.dma(a0c[:, d, :], g.rwkv_a0[l, d].rearrange("(j p) -> p j", p=128), w=[b_a0c], allow_slow_non_contiguous=True)
            k.dma(wup[0:96, d, :], g.rwkv_w_up[l, d], w=[b_wup], q=k.POOL)
            k.dma(aup[0:96, d, :], g.rwkv_a_up[l, d], w=[b_aup], q=k.POOL)
            k.dma(gup[:, d, :], g.rwkv_g_up[l, d * 128:(d + 1) * 128, :], w=[b_gup], q=k.POOL)
        k.dma(kkc[:], g.rwkv_k_k[l].rearrange("(j p) -> p j", p=128), w=[b_kkc], allow_slow_non_contiguous=True)
        k.dma(kac[:], g.rwkv_k_a[l].rearrange("(j p) -> p j", p=128), w=[b_kac], allow_slow_non_contiguous=True)
        k.dma(rkc[:], g.rwkv_r_k[l].rearrange("h n -> (h n)").rearrange("(j p) -> p j", p=128), w=[b_rkc],
              allow_slow_non_contiguous=True)
        k.dma(onesbd[:], g.onesbd, w=[b_onesbd])
        k.dma(tri[:], g.trimask, w=[b_tri])
        k.dma(idf[:], g.ident, w=[b_idf])
        k.op(k.DVE, lambda e: e.tensor_tensor(muc3[:], mu3[:, 0, :], mu3[:, 1, :], ALU.add), r=[b_mu3], w=[b_muc3])
        k.op(k.DVE, lambda e: e.tensor_scalar(muc3[:], muc3[:], -1.0, 1.0, ALU.mult, ALU.add), r=[b_muc3], w=[b_muc3])
        k.op(k.DVE, lambda e: e.tensor_tensor(mucl[:], mul[:, 0, :], mul[:, 1, :], ALU.add), r=[b_mul], w=[b_mucl])
        k.op(k.DVE, lambda e: e.tensor_scalar(mucl[:], mucl[:], -1.0, 1.0, ALU.mult, ALU.add), r=[b_mucl], w=[b_mucl])
        k.op(k.DVE, lambda e: e.tensor_scalar(omka[:], kac[:], -1.0, 1.0, ALU.mult, ALU.add), r=[b_kac], w=[b_omka])

        def shiftmix(dst, b_dst, src, b_src, mc, m0, m1, bm, npart):
            k.op(k.ACT, lambda e: e.activation(dst[0:npart, :], src[0:npart, 1:W + 1], AF.Copy, scale=mc[0:npart]),
                 r=[b_src] + bm, w=[b_dst])
            k.op(k.DVE, lambda e: e.scalar_tensor_tensor(dst[0:npart, :], src[0:npart, 0:W], m0[0:npart], dst[0:npart, :],
                                                         ALU.mult, ALU.add), r=[b_src, b_dst] + bm, w=[b_dst])
            k.op(k.DVE, lambda e: e.scalar_tensor_tensor(dst[0:npart, :], src[0:npart, 2:W + 2], m1[0:npart], dst[0:npart, :],
                                                         ALU.mult, ALU.add), r=[b_src, b_dst] + bm, w=[b_dst])

        bml = [b_mul, b_mucl]
        bm3 = [b_mu3, b_muc3]
        for tb in range(T // W):
            t0 = tb * W
            k.dma(zxw[0:96, :], g.zrT[6144:6240, t0:t0 + W + 2], r=[b_zrT], w=[b_zxw])
            k.dma(zxa[0:96, :], g.zrT[6240:6336, t0:t0 + W + 2], r=[b_zrT], w=[b_zxa])
            k.dma(zxg[:], g.zrT[6336:6592, t0:t0 + W + 2].rearrange("(j p) t -> p j t", p=128), r=[b_zrT], w=[b_zxg])
            shiftmix(lt, b_lt, zxw, b_zxw, mucl[:, 0:1], mul[:, 0, 0:1], mul[:, 1, 0:1], bml, 96)
            k.op(k.ACT, lambda e: e.activation(tw[0:96, :], lt[0:96, :], AF.Tanh), r=[b_lt], w=[b_tw])
            shiftmix(lt, b_lt, zxa, b_zxa, mucl[:, 1:2], mul[:, 0, 1:2], mul[:, 1, 1:2], bml, 96)
            k.op(k.ACT, lambda e: e.copy(xab[0:96, :], lt[0:96, :]), r=[b_lt], w=[b_xab])
            for jj in range(2):
                shiftmix(lt, b_lt, zxg[:, jj, :], b_zxg, mucl[:, 2 + jj:3 + jj], mul[:, 0, 2 + jj:3 + jj],
                         mul[:, 1, 2 + jj:3 + jj], bml, 128)
                k.op(k.ACT, lambda e: e.activation(sgb[:, jj, :], lt[:], AF.Sigmoid), r=[b_lt], w=[b_sgb])
            for hp in range(16):
                cs = slice(hp * 128, (hp + 1) * 128)
                for i3 in range(3):
                    row0 = i3 * 2048 + hp * 128
                    k.dma(zin[i3][0][:], g.zrT[row0:row0 + 128, t0:t0 + W + 2], r=[b_zrT], w=[zin[i3][1]])
                    col = i3 * 16 + hp
                    shiftmix(zs[i3][0], zs[i3][1], zin[i3][0], zin[i3][1], muc3[:, col:col + 1], mu3[:, 0, col:col + 1],
                             mu3[:, 1, col:col + 1], bm3, 128)
                (r_s, b_rs), (k_s, b_ks), (v_s, b_vs) = zs
                k.op(k.DVE, lambda e: e.tensor_tensor(v_s[:], v_s[:], tmb[:, t0:t0 + W], ALU.mult), r=[b_vs, b_tmb], w=[b_vs])
                k.op(k.ACT, lambda e: e.copy(vb[:], v_s[:]), r=[b_vs], w=[b_vb])
                k.dma(g.rk_vT[hp, :, t0:t0 + W], vb[:], r=[b_vb], w=[b_vT])
                for d in range(2):
                    pw, b_pw = PS.get()
                    k.op(k.PE, lambda e: e.matmul(pw[:], wup[0:96, d, cs], tw[0:96, :], start=True, stop=True),
                         r=[b_wup, b_tw], w=[b_pw])
                    k.op(k.ACT, lambda e: e.activation(ld[d][0][:], pw[:], AF.Sigmoid, bias=w0c[:, d, hp:hp + 1]),
                         r=[b_pw, b_w0c], w=[ld[d][1]])
                    pa, b_pa = PS.get()
                    k.op(k.PE, lambda e: e.matmul(pa[:], aup[0:96, d, cs], xab[0:96, :], start=True, stop=True),
                         r=[b_aup, b_xab], w=[b_pa])
                    k.op(k.ACT, lambda e: e.activation(aa[d][0][:], pa[:], AF.Sigmoid, bias=a0c[:, d, hp:hp + 1]),
                         r=[b_pa, b_a0c], w=[aa[d][1]])
                pg, b_pg = PS.get()
                k.op(k.PE, lambda e: e.matmul(pg[:], gup[:, 0, cs], sgb[:, 0, :], start=True, stop=False),
                     r=[b_gup, b_sgb], w=[b_pg], inc=False)
                k.op(k.PE, lambda e: e.matmul(pg[:], gup[:, 1, cs], sgb[:, 1, :], start=False, stop=True),
                     r=[b_gup, b_sgb], w=[b_pg])
                k.op(k.ACT, lambda e: e.copy(gf[:], pg[:]), r=[b_pg], w=[b_gf])
                k.dma(g.rk_g[hp, :, t0:t0 + W], gf[:], r=[b_gf], w=[b_g])
                k.op(k.DVE, lambda e: e.tensor_scalar(kk[:], k_s[:], kkc[:, hp:hp + 1], None, ALU.mult), r=[b_ks, b_kkc], w=[b_kk])
                k.op(k.DVE, lambda e: e.tensor_tensor(sq[:], kk[:], kk[:], ALU.mult), r=[b_kk], w=[b_sq])
                pss, b_pss = PS.get()
                k.op(k.PE, lambda e: e.matmul(pss[:], onesbd[:], sq[:], start=True, stop=True), r=[b_onesbd, b_sq], w=[b_pss])
                k.op(k.ACT, lambda e: e.sqrt(rn[:], pss[:]), r=[b_pss], w=[b_rn])
                k.op(k.DVE, lambda e: e.tensor_scalar_max(rn[:], rn[:], 1e-12), r=[b_rn], w=[b_rn])
                k.op(k.DVE, lambda e: e.reciprocal(rn[:], rn[:]), r=[b_rn], w=[b_rn])
                k.op(k.DVE, lambda e: e.tensor_tensor(kk[:], kk[:], rn[:], ALU.mult), r=[b_kk, b_rn], w=[b_kk])
                for d in range(2):
                    k.op(k.DVE, lambda e: e.tensor_scalar(tk[:], aa[d][0][:], kac[:, hp:hp + 1], omka[:, hp:hp + 1], ALU.mult, ALU.add),
                         r=[aa[d][1], b_kac, b_omka], w=[b_tk])
                    k.op(k.DVE, lambda e: e.tensor_tensor(kd[d][0][:], k_s[:], tk[:], ALU.mult), r=[b_ks, b_tk], w=[kd[d][1]])
                k.op(k.DVE, lambda e: e.tensor_tensor(bt[:], kd[0][0][:], kd[1][0][:], ALU.add), r=[kd[0][1], kd[1][1]], w=[b_bt])
                k.op(k.DVE, lambda e: e.tensor_tensor(bt[:], bt[:], r_s[:], ALU.mult), r=[b_bt, b_rs], w=[b_bt])
                k.op(k.DVE, lambda e: e.tensor_scalar(bt[:], bt[:], rkc[:, hp:hp + 1], None, ALU.mult), r=[b_bt, b_rkc], w=[b_bt])
                pb, b_pb = PS.get()
                k.op(k.PE, lambda e: e.matmul(pb[:], onesbd[:], bt[:], start=True, stop=True), r=[b_onesbd, b_bt], w=[b_pb])
                k.op(k.DVE, lambda e: e.tensor_tensor(bv[:], pb[:], v_s[:], ALU.mult), r=[b_pb, b_vs], w=[b_bv_t])
                k.dma(g.rk_bv[hp, :, t0:t0 + W], bv[:], r=[b_bv_t], w=[b_bv])
                for d in range(2):
                    ldd, b_ldd = ld[d]
                    pT, b_pT = PS.get()
                    for ci in range(4):
                        k.op(k.PE, lambda e: e.transpose(pT[:, ci * 128:(ci + 1) * 128], ldd[:, ci * 128:(ci + 1) * 128], idf[:]),
                             r=[b_ldd, b_idf], w=[b_pT], inc=(ci == 3))
                    k.op(k.DVE, lambda e: e.tensor_copy(ldT[:], pT[:]), r=[b_pT], w=[b_ldT])
                    pc, b_pc = PS.get()
                    for ci in range(4):
                        k.op(k.PE, lambda e: e.matmul(pc[:, ci * 128:(ci + 1) * 128], ldT[:, ci * 128:(ci + 1) * 128],
                                                      tri[:, 1 if d == 0 else 3, :], start=True, stop=True),
                             r=[b_ldT, b_tri], w=[b_pc], inc=(ci == 3))
                    k.op(k.ACT, lambda e: e.mul(cum[:], pc[:], -LWS), r=[b_pc], w=[b_cum])
                    k.op(k.ACT, lambda e: e.activation(ecp[:], cum[:], AF.Exp), r=[b_cum], w=[b_ecp])
                    k.op(k.ACT, lambda e: e.activation(ecm[:], cum[:], AF.Exp, scale=-1.0), r=[b_cum], w=[b_ecm])
                    k.op(k.DVE, lambda e: e.scalar_tensor_tensor(cex[:], ldd[:], LWS, cum[:], ALU.mult, ALU.add),
                         r=[b_ldd, b_cum], w=[b_cex])
                    k.op(k.ACT, lambda e: e.activation(cex[:], cex[:], AF.Exp), r=[b_cex], w=[b_cex])
                    c3 = cum[:].rearrange("p (c t) -> p c t", c=4)
                    endcol = 127 if d == 0 else 0
                    k.op(k.DVE, lambda e: e.tensor_copy(tot[:], c3[:, :, endcol]), r=[b_cum], w=[b_tot])
                    k.op(k.ACT, lambda e: e.activation(gcv[:], tot[:], AF.Exp), r=[b_tot], w=[b_gcv])
                    k.dma(g.rk_gc[d, hp, :, tb * 4:(tb + 1) * 4], gcv[:], r=[b_gcv], w=[b_gc])
                    for ci in range(4):
                        k.op(k.ACT, lambda e: e.activation(ect[:, ci * 128:(ci + 1) * 128], cum[:, ci * 128:(ci + 1) * 128], AF.Exp,
                                                           bias=tot[:, ci:ci + 1], scale=-1.0),
                             r=[b_cum, b_tot], w=[b_ect])
                    k.op(k.DVE, lambda e: e.tensor_tensor(ba[:], kk[:], aa[d][0][:], ALU.mult), r=[b_kk, aa[d][1]], w=[b_ba])
                    prods = [
                        (0, lambda e, o: e.scalar_tensor_tensor(o[:], kk[:], -1.0, cex[:], ALU.mult, ALU.mult), [b_kk, b_cex]),
                        (1, lambda e, o: e.tensor_tensor(o[:], r_s[:], ecp[:], ALU.mult), [b_rs, b_ecp]),
                        (2, lambda e, o: e.tensor_tensor(o[:], ba[:], ecm[:], ALU.mult), [b_ba, b_ecm]),
                        (3, lambda e, o: e.tensor_tensor(o[:], kd[d][0][:], ecm[:], ALU.mult), [kd[d][1], b_ecm]),
                        (4, lambda e, o: e.tensor_tensor(o[:], ba[:], ect[:], ALU.mult), [b_ba, b_ect]),
                        (5, lambda e, o: e.tensor_tensor(o[:], kd[d][0][:], ect[:], ALU.mult), [kd[d][1], b_ect]),
                    ]
                    for (oi, fn, rr) in prods:
                        o, b_o = outs.get()
                        k.op(k.DVE, lambda e: fn(e, o), r=rr, w=[b_o])
                        k.dma(g.rk_ops[d, oi, hp, :, t0:t0 + W], o[:], r=[b_o], w=[b_ops])


def phase_rwkv_scan(g, l):
    nc, k = g.nc, g.k
    b_ops = g.bufs.setdefault("rk_ops", Buf("rk_ops", True))
    b_gc = g.bufs.setdefault("rk_gc", Buf("rk_gc", True))
    b_vT = g.bufs.setdefault("rk_vT", Buf("rk_vT", True))
    b_y = g.bufs.setdefault("rk_y", Buf("rk_y", True))
    k.barrier()
    NCH = T // 128
    with ExitStack() as es:
        def tile(name, shape, dt=F32):
            return k.sb(es, "rs_" + name, shape, dt), Buf()

        tri, b_tri = tile("tri", [128, 4, 128])
        idf, b_idf = tile("idf", [128, 128])
        idb, b_idb = tile("idb", [128, 128], BF16)
        gcs, b_gcs = tile("gcs", [128, 2, 16, NCH])
        S = [[tile(f"S{d}_{p}", [128, 128]) for p in range(16)] for d in range(2)]
        Sb = [[tile(f"Sb{d}_{p}", [128, 128], BF16) for p in range(16)] for d in range(2)]
        G = 4
        slots = []
        for si in range(G):
            sl = {}
            sl["OPS"] = TP(k, es, f"rs{si}_ops", [128, 6, 128], BF16, 3)
            sl["VTp"] = TP(k, es, f"rs{si}_vT", [128, 128], BF16, 3)
            sl["TOK"] = TP(k, es, f"rs{si}_tok", [128, 3, 128], BF16, 2)
            sl["A1"] = TP(k, es, f"rs{si}_a1", [128, 256], BF16, 4)
            sl["MN"] = TP(k, es, f"rs{si}_mn", [128, 128], F32, 8)
            sl["M32P"] = TP(k, es, f"rs{si}_m32", [128, 128], F32, 4)
            sl["PP"] = TP(k, es, f"rs{si}_pp", [128, 128], BF16, 6)
            sl["PP32"] = TP(k, es, f"rs{si}_pp32", [128, 128], F32, 4)
            sl["NNP"] = TP(k, es, f"rs{si}_nn", [128, 128], F32, 4)
            sl["XU"] = TP(k, es, f"rs{si}_xu", [128, 128], BF16, 4)
            sl["YO"] = TP(k, es, f"rs{si}_yo", [128, 128], F32, 3)
            slots.append(sl)
        PS = TP(k, es, "rs_ps", [128, 512], F32, 6, psum=True)
        PSB = TP(k, es, "rs_psb", [128, 1024], BF16, 2, psum=True)

        k.dma(tri[:], g.trimask, w=[b_tri])
        k.dma(idf[:], g.ident, w=[b_idf])
        k.op(k.DVE, lambda e: e.tensor_copy(idb[:], idf[:]), r=[b_idf], w=[b_idb])
        k.dma(gcs[:], g.rk_gc.rearrange("d j p c -> p d j c"), r=[b_gc], w=[b_gcs])
        for d in range(2):
            for p in range(16):
                k.op(k.DVE, lambda e: e.memset(S[d][p][0][:], 0.0), w=[S[d][p][1]])
                k.op(k.DVE, lambda e: e.memset(Sb[d][p][0][:], 0.0), w=[Sb[d][p][1]])

        ecnt = [0]

        def evac(dst, b_dst, src, b_src, extra_r=()):
            ecnt[0] += 1
            if ecnt[0] % 2 == 0:
                k.op(k.ACT, lambda e: e.copy(dst, src), r=[b_src] + list(extra_r), w=[b_dst])
            else:
                k.op(k.DVE, lambda e: e.tensor_copy(dst, src), r=[b_src] + list(extra_r), w=[b_dst])


        def chain(step, d, p, sl):
            OPS = sl["OPS"]
            VTp = sl["VTp"]
            TOK = sl["TOK"]
            A1 = sl["A1"]
            MN = sl["MN"]
            M32P = sl["M32P"]
            PP = sl["PP"]
            PP32 = sl["PP32"]
            NNP = sl["NNP"]
            XU = sl["XU"]
            YO = sl["YO"]
            n = step if d == 0 else NCH - 1 - step
            ts = slice(n * 128, (n + 1) * 128)
            m_strict = 0 if d == 0 else 2
            m_incl = 1 if d == 0 else 3
            m_nn = 2 if d == 0 else 0
            ops, b_o = OPS.get()
            k.dma(ops[:], g.rk_ops[d, :, p, :, ts].rearrange("i p t -> p i t"), r=[b_ops], w=[b_o])
            vT, b_v = VTp.get()
            k.dma(vT[:], g.rk_vT[p, :, ts], r=[b_vT], w=[b_v])
            aT, rT, bT, kT, btT, ktT = (ops[:, i, :] for i in range(6))
            St, b_S = S[d][p]
            Sbt, b_Sb = Sb[d][p]
            ptr_b, b_ptr = PSB.get()
            for i3, src in enumerate((vT[:], btT, ktT)):
                k.op(k.PE, lambda e: e.transpose(ptr_b[:, i3 * 128:(i3 + 1) * 128], src, idb[:]),
                     r=[b_v, b_o, b_idb], w=[b_ptr], inc=(i3 == 2))
            tok, b_tok = TOK.get()
            evac(tok[:].rearrange("p a b -> p (a b)"), b_tok, ptr_b[:, 0:384], b_ptr)
            Vt, Bt, Kt = tok[:, 0, :], tok[:, 1, :], tok[:, 2, :]
            yield
            if g.scan_cut <= 1:
                return
            KA, BA, NN, MM = [], [], [], []
            for h in range(2):
                hs = slice(h * 64, (h + 1) * 64)
                pk, b_pk = PS.get()
                k.op(k.PE, lambda e: e.matmul(pk[:, 0:256], kT[hs, :], ops[hs, 0:2, :].rearrange("p a b -> p (a b)"),
                                              start=True, stop=True), r=[b_o], w=[b_pk])
                ka, b_ka = A1.get()
                k.op(k.DVE, lambda e: e.tensor_tensor(ka[:, 0:128], pk[:, 0:128], tri[:, m_strict, :], ALU.mult),
                     r=[b_pk, b_tri], w=[b_ka])
                k.op(k.DVE, lambda e: e.tensor_tensor(ka[:, 128:256], pk[:, 128:256], tri[:, m_incl, :], ALU.mult),
                     r=[b_pk, b_tri, b_ka], w=[b_ka])
                pb, b_pb = PS.get()
                k.op(k.PE, lambda e: e.matmul(pb[:, 0:256], bT[hs, :], ops[hs, 0:2, :].rearrange("p a b -> p (a b)"),
                                              start=True, stop=True), r=[b_o], w=[b_pb])
                bb, b_bb = A1.get()
                m32, b_m32 = M32P.get()
                k.op(k.DVE, lambda e: e.tensor_tensor(m32[:], pb[:, 0:128], tri[:, m_strict, :], ALU.mult),
                     r=[b_pb, b_tri], w=[b_m32])
                k.op(k.DVE, lambda e: e.tensor_tensor(bb[:, 128:256], pb[:, 128:256], tri[:, m_incl, :], ALU.mult),
                     r=[b_pb, b_tri], w=[b_bb])
                pn, b_pn = PS.get()
                k.op(k.PE, lambda e: e.matmul(pn[:, 0:128], aT[hs, :], bT[hs, :], start=True, stop=True), r=[b_o], w=[b_pn])
                nn, b_nn = NNP.get()
                k.op(k.DVE, lambda e: e.tensor_tensor(nn[:], pn[:, 0:128], tri[:, m_nn, :], ALU.mult),
                     r=[b_pn, b_tri], w=[b_nn])
                KA.append((ka, b_ka))
                BA.append((bb, b_bb))
                MM.append((m32, b_m32))
                NN.append((nn, b_nn))
            yield
            if g.scan_cut <= 2:
                return
            Pm = []
            for h in range(2):
                Mc, b_Mc = MM[h][0][:], MM[h][1]
                Nc, b_Nc = NN[h][0][:], NN[h][1]
                P32, b_P32 = PP32.get()
                k.op(k.DVE, lambda e: e.tensor_tensor(P32[:], Mc, idf[:], ALU.add), r=[b_Mc, b_idf], w=[b_P32])
                for lev in range(1, 7):
                    pmA, b_pmA = PS.get()
                    k.op(k.PE, lambda e: e.matmul(pmA[:, 0:128], Nc, Mc, start=True, stop=True), r=[b_Nc, b_Mc], w=[b_pmA])
                    pmB, b_pmB = PS.get()
                    k.op(k.PE, lambda e: e.matmul(pmB[:, 0:128], Mc, Nc, start=True, stop=True), r=[b_Nc, b_Mc], w=[b_pmB])
                    m2, b_m2 = MN.get()
                    n2, b_n2 = MN.get()
                    k.op(k.ACT, lambda e: e.copy(m2[:], pmA[:, 0:128]), r=[b_pmA], w=[b_m2])
                    k.op(k.DVE, lambda e: e.tensor_copy(n2[:], pmB[:, 0:128]), r=[b_pmB], w=[b_n2])
                    Mc, b_Mc, Nc, b_Nc = m2[:], b_m2, n2[:], b_n2
                    yield
                    pq, b_pq = PS.get()
                    k.op(k.PE, lambda e: e.matmul(pq[:, 0:128], Nc, P32[:], start=True, stop=True), r=[b_Nc, b_P32], w=[b_pq])
                    P32n, b_P32n = PP32.get()
                    k.op(k.DVE, lambda e: e.tensor_tensor(P32n[:], pq[:, 0:128], P32[:], ALU.add), r=[b_pq, b_P32], w=[b_P32n])
                    P32, b_P32 = P32n, b_P32n
                    yield
                P, b_P = PP.get()
                k.op(k.ACT, lambda e: e.copy(P[:], P32[:]), r=[b_P32], w=[b_P])
                Pm.append((P, b_P))
            yield
            if g.scan_cut <= 3:
                return
            px, b_px = PS.get()
            k.op(k.PE, lambda e: e.matmul(px[:, 0:128], aT, Sbt[:], start=True, stop=False), r=[b_o, b_Sb], w=[b_px], inc=False)
            for h in range(2):
                hc = slice(h * 64, (h + 1) * 64)
                k.op(k.PE, lambda e: e.matmul(px[:, hc], KA[h][0][:, 0:128], Vt[:, hc], start=False, stop=(h == 1)),
                     r=[KA[h][1], b_tok], w=[b_px], inc=(h == 1))
            X, b_X = XU.get()
            evac(X[:], b_X, px[:, 0:128], b_px)
            yield
            pu, b_pu = PS.get()
            for h in range(2):
                hc = slice(h * 64, (h + 1) * 64)
                k.op(k.PE, lambda e: e.matmul(pu[:, hc], Pm[h][0][:], X[:, hc], start=True, stop=True),
                     r=[Pm[h][1], b_X], w=[b_pu], inc=(h == 1))
            U, b_U = XU.get()
            evac(U[:], b_U, pu[:, 0:128], b_pu)
            yield
            if g.scan_cut <= 4:
                return
            py, b_py = PS.get()
            k.op(k.PE, lambda e: e.matmul(py[:, 0:128], rT, Sbt[:], start=True, stop=False), r=[b_o, b_Sb], w=[b_py], inc=False)
            for h in range(2):
                hc = slice(h * 64, (h + 1) * 64)
                k.op(k.PE, lambda e: e.matmul(py[:, hc], BA[h][0][:, 128:256], U[:, hc], start=False, stop=False),
                     r=[BA[h][1], b_U], w=[b_py], inc=False)
                k.op(k.PE, lambda e: e.matmul(py[:, hc], KA[h][0][:, 128:256], Vt[:, hc], start=False, stop=(h == 1)),
                     r=[KA[h][1], b_tok], w=[b_py], inc=(h == 1))
            yo, b_yo = YO.get()
            evac(yo[:], b_yo, py[:, 0:128], b_py)
            k.dma(g.rk_y[d, ts, p * 128:(p + 1) * 128], yo[:], r=[b_yo], w=[b_y])
            yield
            if g.scan_cut <= 5:
                return
            psn, b_psn = PS.get()
            k.op(k.PE, lambda e: e.matmul(psn[:, 0:128], Bt, U[:], start=True, stop=False), r=[b_tok, b_U], w=[b_psn], inc=False)
            k.op(k.PE, lambda e: e.matmul(psn[:, 0:128], Kt, Vt, start=False, stop=True), r=[b_tok], w=[b_psn])
            for h in range(2):
                hs = slice(h * 64, (h + 1) * 64)
                k.op(k.DVE, lambda e: e.scalar_tensor_tensor(St[hs, hs], St[hs, hs], gcs[hs, d, p, n:n + 1], psn[hs, hs],
                                                             ALU.mult, ALU.add),
                     r=[b_S, b_gcs, b_psn], w=[b_S])
            k.op(k.ACT, lambda e: e.copy(Sbt[:], St[:]), r=[b_S], w=[b_Sb])

        npairs = 16 if g.scan_cut >= 99 else 1
        for step in range(NCH if g.scan_steps is None else g.scan_steps):
            combos = [(d, p) for d in range(2) for p in range(npairs)]
            for gi in range(0, len(combos), G):
                gens = [chain(step, d, p, slots[i]) for i, (d, p) in enumerate(combos[gi:gi + G])]
                while gens:
                    for gen in list(gens):
                        try:
                            next(gen)
                        except StopIteration:
                            gens.remove(gen)


def phase_rwkv_post(g, l):
    nc, k = g.nc, g.k
    b_y = g.bufs.setdefault("rk_y", Buf("rk_y", True))
    b_g = g.bufs.setdefault("rk_g", Buf("rk_g", True))
    b_bv = g.bufs.setdefault("rk_bv", Buf("rk_bv", True))
    b_mixT = g.bufs.setdefault("mixT", Buf("mixT", True))
    k.barrier()
    with ExitStack() as es:
        def tile(name, shape, dt=F32):
            return k.sb(es, "rq_" + name, shape, dt), Buf()

        idf, b_idf = tile("idf", [128, 128])
        lng, b_lng = tile("lng", [128, 16])
        lnb, b_lnb = tile("lnb", [128, 16])
        y0 = [tile(f"y0{i}", [128, 2048]) for i in range(2)]
        y1 = [tile(f"y1{i}", [128, 2048]) for i in range(2)]
        sqv, b_sqv = tile("sqv", [128, 2048])
        s1, b_s1 = tile("s1", [128, 32])
        s2, b_s2 = tile("s2", [128, 32])
        gt = [tile(f"g{i}", [128, 16, 128]) for i in range(2)]
        bvt = [tile(f"bv{i}", [128, 16, 128]) for i in range(2)]
        yT, b_yT = tile("yT", [128, 16, 128])
        ob = [tile(f"ob{i}", [128, 16, 128], BF16) for i in range(2)]
        PS = TP(k, es, "rq_ps", [128, 512], F32, 4, psum=True)
        k.dma(idf[:], g.ident, w=[b_idf])
        k.dma(lng[:], g.rwkv_ln_g[l].rearrange("(j p) -> p j", p=128), w=[b_lng], allow_slow_non_contiguous=True)
        k.dma(lnb[:], g.rwkv_ln_b[l].rearrange("(j p) -> p j", p=128), w=[b_lnb], allow_slow_non_contiguous=True)
        for i in range(NT):
            j = i % 2
            ts = slice(i * 128, (i + 1) * 128)
            ya, b_ya = y0[j]
            yb, b_yb = y1[j]
            k.dma(ya[:], g.rk_y[0, ts, :], r=[b_y], w=[b_ya])
            k.dma(yb[:], g.rk_y[1, ts, :], r=[b_y], w=[b_yb])
            k.dma(gt[j][0][:], g.rk_g[:, :, ts].rearrange("j p t -> p j t"), r=[b_g], w=[gt[j][1]])
            k.dma(bvt[j][0][:], g.rk_bv[:, :, ts].rearrange("j p t -> p j t"), r=[b_bv], w=[bvt[j][1]])
            k.op(k.DVE, lambda e: e.tensor_tensor(ya[:], ya[:], yb[:], ALU.add), r=[b_ya, b_yb], w=[b_ya])
            y3 = ya[:].rearrange("p (h n) -> p h n", n=64)
            k.op(k.DVE, lambda e: e.reduce_sum(s1[:], y3, AX.X), r=[b_ya], w=[b_s1])
            k.op(k.DVE, lambda e: e.tensor_scalar_mul(s1[:], s1[:], 1.0 / 64), r=[b_s1], w=[b_s1])
            k.op(k.DVE, lambda e: e.tensor_tensor(y3, y3, s1[:].unsqueeze(2).to_broadcast([128, 32, 64]), ALU.subtract),
                 r=[b_ya, b_s1], w=[b_ya])
            k.op(k.ACT, lambda e: e.activation(sqv[:], ya[:], AF.Square), r=[b_ya], w=[b_sqv])
            k.op(k.DVE, lambda e: e.reduce_sum(s2[:], sqv[:].rearrange("p (h n) -> p h n", n=64), AX.X), r=[b_sqv], w=[b_s2])
            k.op(k.DVE, lambda e: e.tensor_scalar(s2[:], s2[:], 1.0 / 64, GN_EPS, ALU.mult, ALU.add), r=[b_s2], w=[b_s2])
            k.op(k.ACT, lambda e: e.sqrt(s2[:], s2[:]), r=[b_s2], w=[b_s2])
            k.op(k.DVE, lambda e: e.reciprocal(s2[:], s2[:]), r=[b_s2], w=[b_s2])
            k.op(k.DVE, lambda e: e.tensor_tensor(y3, y3, s2[:].unsqueeze(2).to_broadcast([128, 32, 64]), ALU.mult),
                 r=[b_ya, b_s2], w=[b_ya])
            for q in range(4):
                pt, b_pt = PS.get()
                for jj in range(4):
                    p = q * 4 + jj
                    k.op(k.PE, lambda e: e.transpose(pt[:, jj * 128:(jj + 1) * 128], ya[:, p * 128:(p + 1) * 128], idf[:]),
                         r=[b_ya, b_idf], w=[b_pt], inc=(jj == 3))
                for jj in range(4):
                    p = q * 4 + jj
                    k.op(k.ACT, lambda e: e.activation(yT[:, p, :], pt[:, jj * 128:(jj + 1) * 128], AF.Identity,
                                                       bias=lnb[:, p:p + 1], scale=lng[:, p:p + 1]),
                         r=[b_pt, b_lng, b_lnb], w=[b_yT])
            k.op(k.DVE, lambda e: e.tensor_tensor(yT[:], yT[:], bvt[j][0][:], ALU.add), r=[b_yT, bvt[j][1]], w=[b_yT])
            k.op(k.DVE, lambda e: e.tensor_tensor(ob[j][0][:], yT[:], gt[j][0][:], ALU.mult), r=[b_yT, gt[j][1]], w=[ob[j][1]])
            k.dma(g.mixT[16:32, :, ts].rearrange("j p t -> p j t"), ob[j][0][:], r=[ob[j][1]], w=[b_mixT])


def phase_rwkv(g, l):
    phase_rwkv_prep(g, l)
    phase_rwkv_scan(g, l)
    phase_rwkv_post(g, l)


def kernel(**inputs):
    nc = build_program()
    in_maps = []
    xs = np.asarray(inputs["x_sample"], dtype=np.float32)
    xp = np.asarray(inputs["x_prompt"], dtype=np.float32)
    ns = xs.shape[1]
    for core in range(8):
        if core < 4:
            x = xp[core]
            c = np.asarray(inputs["c_prompt"], dtype=np.float32)[core]
            tm = np.ones(T, np.float32)
        else:
            x = np.zeros((T, D), np.float32)
            x[:ns] = xs[core - 4]
            c = np.asarray(inputs["c_sample"], dtype=np.float32)[core - 4]
            tm = np.zeros(T, np.float32)
            tm[:ns] = 1.0
        in_maps.append(make_inputs(inputs, x, c, tm))
    res = run_bass_kernel_spmd(nc, in_maps, core_ids=list(range(8)))
    y_prompt = np.stack([np.asarray(res.results[i]["y"], dtype=np.float32) for i in range(4)])
    y_sample = np.stack([np.asarray(res.results[4 + i]["y"], dtype=np.float32)[:ns] for i in range(4)])
    return (y_prompt, y_sample)
```
